# Optimizing a Trainium2 kernel written in Bass

```python
import jax, jax.numpy as jnp
from jax import lax
import numpy as np

D_MODEL = 2048
BATCH = 4
SEQ = 4096
DEPTH = 2

N_META = 16
HEAD_DIM = 128
N_HEADS_SB = D_MODEL // (2 * HEAD_DIM)
N_HEADS_FOX = D_MODEL // (2 * HEAD_DIM)
W_SB = N_HEADS_SB * HEAD_DIM
W_FOX = N_HEADS_FOX * HEAD_DIM
W_MIX = W_SB + W_FOX
N_IN = 3 * W_SB + 3 * W_FOX + N_HEADS_FOX
D_FF = 11 * D_MODEL // 4
CONV_WIDTH = 3
Q_BLOCK = 128
EPS = 1e-6

kernel_name = "hymba_stickbreak_fox_convffn"


def rms_norm(x, g):
    xf = x.astype(jnp.float32)
    y = xf * lax.rsqrt(jnp.mean(xf * xf, axis=-1, keepdims=True) + EPS)
    return (y * g.astype(jnp.float32)).astype(x.dtype)


def block_bounds():
    bounds = [(0, N_META)]
    for i in range(SEQ // Q_BLOCK):
        bounds.append((N_META + i * Q_BLOCK, N_META + (i + 1) * Q_BLOCK))
    return bounds


def stick_breaking_attention(q, k, v):
    scale = HEAD_DIM ** -0.5
    outs = []
    for qs, qe in block_bounds():
        z = jnp.einsum('bqhd,bkhd->bhqk', q[:, qs:qe], k[:, :qe]).astype(jnp.float32) * scale
        t_pos = jnp.arange(qs, qe)[:, None]
        s_pos = jnp.arange(qe)[None, :]
        before = s_pos < t_pos
        log_keep = jnp.where(before, -jax.nn.softplus(z), 0.0)
        log_keep_between = lax.cumsum(log_keep, axis=3, reverse=True) - log_keep
        a = jnp.where(before, jnp.exp(jax.nn.log_sigmoid(z) + log_keep_between), 0.0)
        outs.append(jnp.einsum('bhqk,bkhd->bqhd', a.astype(v.dtype), v[:, :qe]))
    return jnp.concatenate(outs, axis=1)


def forgetting_attention(q, k, v, log_f):
    scale = HEAD_DIM ** -0.5
    c = jnp.transpose(jnp.cumsum(log_f, axis=1), (0, 2, 1))
    outs = []
    for qs, qe in block_bounds():
        logits = jnp.einsum('bqhd,bkhd->bhqk', q[:, qs:qe], k[:, :qe]).astype(jnp.float32) * scale
        logits = logits + (c[:, :, qs:qe, None] - c[:, :, None, :qe])
        t_pos = jnp.arange(qs, qe)[:, None]
        s_pos = jnp.arange(qe)[None, :]
        logits = jnp.where(s_pos <= t_pos, logits, -jnp.inf)
        p = jax.nn.softmax(logits, axis=-1)
        outs.append(jnp.einsum('bhqk,bkhd->bqhd', p.astype(v.dtype), v[:, :qe]))
    return jnp.concatenate(outs, axis=1)


def causal_depthwise_conv(a, w, bias):
    c = a.shape[-1]
    out = lax.conv_general_dilated(
        a, w.astype(a.dtype)[:, None, :], window_strides=(1,),
        padding=[(CONV_WIDTH - 1, 0)], dimension_numbers=('NWC', 'WIO', 'NWC'),
        feature_group_count=c)
    return out + bias.astype(a.dtype)


def hybrid_layer(h, g_mix_pre, w_in, b_f, g_sb, g_fox, w_out, g_mix_post,
                 g_ffn_pre, w_up, conv_w, conv_b, w_down, g_ffn_post):
    b, l, _ = h.shape
    u = rms_norm(h, g_mix_pre)
    proj = u @ w_in
    splits = [W_SB, 2 * W_SB, 3 * W_SB, 3 * W_SB + W_FOX, 3 * W_SB + 2 * W_FOX, 3 * W_SB + 3 * W_FOX]
    q_sb, k_sb, v_sb, q_fx, k_fx, v_fx, f_logit = jnp.split(proj, splits, axis=-1)
    heads_sb = lambda t: t.reshape(b, l, N_HEADS_SB, HEAD_DIM)
    heads_fx = lambda t: t.reshape(b, l, N_HEADS_FOX, HEAD_DIM)
    o_sb = stick_breaking_attention(heads_sb(q_sb), heads_sb(k_sb), heads_sb(v_sb))
    log_f = jax.nn.log_sigmoid((f_logit + b_f).astype(jnp.float32))
    o_fx = forgetting_attention(heads_fx(q_fx), heads_fx(k_fx), heads_fx(v_fx), log_f)
    o_sb = rms_norm(o_sb, g_sb).reshape(b, l, W_SB)
    o_fx = rms_norm(o_fx, g_fox).reshape(b, l, W_FOX)
    mix = jnp.concatenate([o_sb, o_fx], axis=-1) @ w_out
    h = h + rms_norm(mix, g_mix_post)
    u = rms_norm(h, g_ffn_pre)
    a = causal_depthwise_conv(u @ w_up, conv_w, conv_b)
    gate, up = jnp.split(a, [D_FF], axis=-1)
    ff = (jax.nn.silu(gate) * up) @ w_down
    return h + rms_norm(ff, g_ffn_post)


def setup_inputs(seed: int = 0) -> dict:
    key = jax.random.key(seed)
    ks = jax.random.split(key, 16)
    f32 = jnp.float32
    nrm = lambda k, shape, s: jax.random.normal(k, shape, f32) * s
    gain = lambda k, shape: 1.0 + 0.02 * jax.random.normal(k, shape, f32)
    return {
        "x": nrm(ks[0], (BATCH, SEQ, D_MODEL), 1.0),
        "meta": nrm(ks[1], (N_META, D_MODEL), 1.0),
        "g_mix_pre": gain(ks[2], (DEPTH, D_MODEL)),
        "w_in": nrm(ks[3], (DEPTH, D_MODEL, N_IN), D_MODEL ** -0.5),
        "b_f": 3.0 + 0.5 * jax.random.normal(ks[4], (DEPTH, N_HEADS_FOX), f32),
        "g_sb": gain(ks[5], (DEPTH, N_HEADS_SB, HEAD_DIM)),
        "g_fox": gain(ks[6], (DEPTH, N_HEADS_FOX, HEAD_DIM)),
        "w_out": nrm(ks[7], (DEPTH, W_MIX, D_MODEL), W_MIX ** -0.5),
        "g_mix_post": gain(ks[8], (DEPTH, D_MODEL)),
        "g_ffn_pre": gain(ks[9], (DEPTH, D_MODEL)),
        "w_up": nrm(ks[10], (DEPTH, D_MODEL, 2 * D_FF), D_MODEL ** -0.5),
        "conv_w": nrm(ks[11], (DEPTH, CONV_WIDTH, 2 * D_FF), CONV_WIDTH ** -0.5),
        "conv_b": nrm(ks[12], (DEPTH, 2 * D_FF), 0.01),
        "w_down": nrm(ks[13], (DEPTH, D_FF, D_MODEL), D_FF ** -0.5),
        "g_ffn_post": gain(ks[14], (DEPTH, D_MODEL)),
    }


def reference(x, meta, g_mix_pre, w_in, b_f, g_sb, g_fox, w_out, g_mix_post,
              g_ffn_pre, w_up, conv_w, conv_b, w_down, g_ffn_post):
    b = x.shape[0]
    meta_b = jnp.broadcast_to(meta[None].astype(x.dtype), (b, N_META, D_MODEL))
    h = jnp.concatenate([meta_b, x], axis=1)
    for i in range(DEPTH):
        h = hybrid_layer(h, g_mix_pre[i], w_in[i], b_f[i], g_sb[i], g_fox[i], w_out[i],
                         g_mix_post[i], g_ffn_pre[i], w_up[i], conv_w[i], conv_b[i],
                         w_down[i], g_ffn_post[i])
    return h[:, N_META:]
```

```python
import numpy as np
import ml_dtypes
import concourse.bass as bass
import concourse.mybir as mybir
from concourse.bass_utils import run_bass_kernel_spmd

F32 = mybir.dt.float32
BF16 = mybir.dt.bfloat16
AF = mybir.ActivationFunctionType
ALU = mybir.AluOpType

D = 2048
KC = 16
NH = 16
HD = 128
NIN = 6152
DFF = 5632
NCH = 44
EPS = 1e-6
NMETA = 16
NPADROWS = 112
SAME_ENGINE_SYNC = True
NDMASEM = 20


class Prog:
    ENGS = ("pe", "act", "dve", "pool", "sp")

    def __init__(self):
        self.ops = []
        self.last_w = {}
        self.readers = {}
        self.final = []
        self.pending = {e: set() for e in self.ENGS}
        self.dma_since = []
        self.last_compute = {}

    def barrier(self):
        deps = set(self.last_compute.values()) | set(self.dma_since)
        for e in self.ENGS:
            self.pending[e] |= deps
        self.dma_since = []

    def add(self, eng, fn, reads=(), writes=(), dma=False, bg=False):
        i = len(self.ops)
        deps = set(self.pending[eng])
        self.pending[eng] = set()
        for b in reads:
            w = self.last_w.get(b)
            if w is not None:
                deps.add(w)
        for b in writes:
            w = self.last_w.get(b)
            if w is not None:
                deps.add(w)
            deps.update(self.readers.get(b, ()))
        keep = set()
        for d in deps:
            p = self.ops[d]
            if (not p["dma"]) and p["eng"] == eng:
                if eng == "pe" or not SAME_ENGINE_SYNC:
                    continue
            keep.add(d)
        self.ops.append(dict(eng=eng, fn=fn, dma=dma, deps=keep))
        if dma:
            if not bg:
                self.dma_since.append(i)
        else:
            self.last_compute[eng] = i
        for b in reads:
            self.readers.setdefault(b, []).append(i)
        for b in writes:
            self.last_w[b] = i
            self.readers[b] = []
        return i

    def emit(self, nc, block, engsem, dmasems):
        ops = self.ops
        dcount = {"sp": 0, "pool": 0}
        hist = {"sp": [], "pool": []}
        for i, op in enumerate(ops):
            if op["dma"]:
                q = op["eng"]
                k = dcount[q]
                dcount[q] += 1
                op["sem"] = dmasems[q][k % NDMASEM]
                op["val"] = 16 * (k // NDMASEM + 1)
                if k >= NDMASEM:
                    op["deps"].add(hist[q][k - NDMASEM])
                hist[q].append(i)
        flagged = set()
        for op in ops:
            for d in op["deps"]:
                if not ops[d]["dma"]:
                    flagged.add(d)
        cnt = {e: 0 for e in self.ENGS}
        for i, op in enumerate(ops):
            if (not op["dma"]) and i in flagged:
                cnt[op["eng"]] += 1
                op["sem"] = engsem[op["eng"]]
                op["val"] = cnt[op["eng"]]
        streams = {e: [] for e in self.ENGS}
        for i, op in enumerate(ops):
            streams[op["eng"]].append(i)
        final = self.final

        def run(engname, e):
            waited = {}
            for i in streams[engname]:
                op = ops[i]
                need = {}
                for d in op["deps"]:
                    p = ops[d]
                    s, v = p["sem"], p["val"]
                    key = id(s)
                    if key not in need or need[key][1] < v:
                        need[key] = (s, v)
                for key, (s, v) in need.items():
                    if waited.get(key, 0) >= v:
                        continue
                    e.wait_ge(s, v)
                    waited[key] = v
                ins = op["fn"](e)
                if op["dma"]:
                    ins.then_inc(op["sem"], 16)
                elif i in flagged:
                    ins.then_inc(op["sem"], 1)
            if engname == "sp":
                for d in final:
                    p = ops[d]
                    e.wait_ge(p["sem"], p["val"])

        @block.tensor
        def _(e):
            run("pe", e)

        @block.scalar
        def _(e):
            run("act", e)

        @block.vector
        def _(e):
            run("dve", e)

        @block.gpsimd
        def _(e):
            run("pool", e)

        @block.sync
        def _(e):
            run("sp", e)


def build_program(NB, NL=2):
    T = NB * 128
    nc = bass.Bass("TRN2", target_bir_lowering=False)
    P = Prog()

    def din(name, shape):
        return nc.dram_tensor(name, list(shape), F32, kind="ExternalInput").ap()

    xp = din("xp", [T, D])
    w_in = din("w_in", [NL, D, NIN])
    w_out = din("w_out", [NL, D, D])
    w_up = din("w_up", [NL, D, 2 * DFF])
    w_down = din("w_down", [NL, DFF, D])
    gains = din("gains", [NL, 4, 128, D])
    ghead = din("ghead", [NL, 128, NH])
    bfrep = din("bfrep", [NL, 128, NB * 8])
    convp = din("convp", [NL, 128, 2 * NCH * 4])
    cmat = din("cmat", [128, 11 * 128])
    selm = din("selm", [8, 8 * 128])
    rowm = din("rowm", [128, NB])
    biask = din("biask", [128, NB * 8])
    OWN0 = (NB - 1) // 2
    assert OWN0 % 4 == 0
    NOUT = NB - 1 - OWN0
    y = nc.dram_tensor("y", [NOUT * 128, D], F32, kind="ExternalOutput").ap()

    Hs = nc.dram_tensor("Hs", [T, D], F32).ap()
    QTd = nc.dram_tensor("QTd", [NH, 128, T], BF16).ap()
    KTd = nc.dram_tensor("KTd", [NH, 128, T], BF16).ap()
    Vd = nc.dram_tensor("Vd", [T, D], BF16).ap()
    OTd = nc.dram_tensor("OTd", [NH, 128, T], BF16).ap()
    U2Td = nc.dram_tensor("U2Td", [128, KC, T], BF16).ap()
    GTd = nc.dram_tensor("GTd", [NCH, 128, T], BF16).ap()
    FFd = nc.dram_tensor("FFd", [T, D], F32).ap()
    WQKb = nc.dram_tensor("WQKb", [NL, 32, 128, KC * 128], BF16).ap()
    WVb = nc.dram_tensor("WVb", [NL, 4, 128, KC * 512], BF16).ap()
    WFb = nc.dram_tensor("WFb", [NL, 128, KC * 8], BF16).ap()
    WOb = nc.dram_tensor("WOb", [NL, KC, 128, D], BF16).ap()
    WUPb = nc.dram_tensor("WUPb", [NL, 2 * NCH, 128, KC * 128], BF16).ap()
    WDNb = nc.dram_tensor("WDNb", [NL, 2, NCH // 4, 128, 4 * 1024], BF16).ap()

    ARENA_F = 46 * 1024
    CONST_F = 2304

    from contextlib import ExitStack
    with ExitStack() as es:
        arena_t = es.enter_context(nc.sbuf_tensor("arena", [128, ARENA_F], F32))
        const_t = es.enter_context(nc.sbuf_tensor("consts", [128, CONST_F], F32))
        psf = [es.enter_context(nc.psum_tensor(f"psf{i}", [128, 512], F32)) for i in range(6)]
        psb = [es.enter_context(nc.psum_tensor(f"psb{i}", [128, 1024], BF16)) for i in range(2)]
        engsem = {e: es.enter_context(nc.semaphore(f"sem_{e}")) for e in Prog.ENGS}
        dmasems = {q: [es.enter_context(nc.semaphore(f"dsem_{q}{i}")) for i in range(NDMASEM)]
                   for q in ("sp", "pool")}
        block = es.enter_context(nc.Block())

        arena_f = arena_t[:]
        arena_b = arena_t[:].bitcast(BF16)
        const_f = const_t[:]
        const_b = const_t[:].bitcast(BF16)

        class Carver:
            def __init__(self, base=0):
                self.off = base

            def f32(self, n):
                o = (self.off + 3) // 4
                self.off = (o + n) * 4
                assert self.off <= ARENA_F * 4, self.off
                return arena_f[:, o:o + n]

            def bf(self, n):
                o = (self.off + 1) // 2
                self.off = (o + n) * 2
                assert self.off <= ARENA_F * 4, self.off
                return arena_b[:, o:o + n]

        CB = const_b[:, 0:1280]
        SELB = const_b[0:8, 1280:2304]
        CFo = 1280
        CF = const_f[:, CFo:CFo + 256]
        MF = const_f[:, CFo + 256:CFo + 512]
        RM = const_f[:, CFo + 530:CFo + 530 + NB]
        E0F = const_f[:, CFo + 600:CFo + 728]
        P.add("sp", lambda e: e.dma_start(out=E0F, in_=cmat[:, 1280:1408]), writes=["E0F"], dma=True)
        IDENT = CB[:, 0:128]
        MASK_LE = CB[:, 128:256]
        NTRI_INCL = CB[:, 384:512]
        NTRI_STRICT = CB[:, 512:640]
        ONESB = CB[:, 640:768]
        ZEROB = CB[:, 896:1024]
        NEG_LT = CB[:, 1024:1152]
        NEG_LE = CB[:, 1152:1280]
        ONESF = CF[:, 0:128]
        TRILEF = CF[:, 128:256]
        MASK_LT_F = MF[:, 0:128]

        P.add("pool", lambda e: e.dma_start(out=CB, in_=cmat[:, 0:1280]), writes=["CB"], dma=True)
        P.add("pool", lambda e: e.dma_start(out=SELB, in_=selm), writes=["SELB"], dma=True)
        P.add("sp", lambda e: e.dma_start(out=CF[:, 0:128], in_=cmat[:, 640:768]), writes=["CF0"], dma=True)
        P.add("sp", lambda e: e.dma_start(out=CF[:, 128:256], in_=cmat[:, 768:896]), writes=["CF1"], dma=True)
        P.add("sp", lambda e: e.dma_start(out=MF[:, 0:128], in_=cmat[:, 256:384]), writes=["MF"], dma=True)
        P.add("sp", lambda e: e.dma_start(out=RM, in_=rowm), writes=["RM"], dma=True)

        ring = {"i": 0}

        def next_bank():
            b = ring["i"] % 6
            ring["i"] += 1
            return b

        groups = []
        b0 = 0
        while b0 < NB:
            nb = min(4, NB - b0)
            groups.append((b0, nb))
            b0 += nb

        def rstd_ops(ss, rs, tag, mul):
            P.add("act", lambda e: e.activation(out=rs, in_=ss, func=AF.Ln, scale=mul, bias=EPSB),
                  reads=[tag + "_ss", "EPSB"], writes=[tag + "_rs"])
            P.add("act", lambda e: e.activation(out=rs, in_=rs, func=AF.Exp, scale=-0.5),
                  reads=[tag + "_rs"], writes=[tag + "_rs"])

        EPSB = const_f[:, CFo + 520:CFo + 521]
        ONEB = const_f[:, CFo + 521:CFo + 522]
        P.add("dve", lambda e: e.memset(EPSB, EPS), writes=["EPSB"])
        P.add("dve", lambda e: e.memset(ONEB, 1.0), writes=["ONEB"])

        def qk_cols(h):
            fox_ = h >= 8
            hh_ = h - 8 if fox_ else h
            return (3072 if fox_ else 0) + hh_ * 128, (4096 if fox_ else 1024) + hh_ * 128

        for l in range(NL):
            wi = w_in[l].rearrange("(k p) c -> p k c", p=128)
            ci = 0
            for h in range(NH):
                for col in qk_cols(h):
                    P.add("pool", lambda e, l=l, ci=ci, col=col, wi=wi: e.dma_start(
                        out=WQKb[l, ci].rearrange("p (k c) -> p k c", k=KC), in_=wi[:, :, col:col + 128]),
                        writes=[("WQKb", l, ci)], dma=True, bg=True)
                    ci += 1
            for c4 in range(4):
                col = (2048 if c4 < 2 else 5120) + (c4 % 2) * 512
                P.add("pool", lambda e, l=l, c4=c4, col=col, wi=wi: e.dma_start(
                    out=WVb[l, c4].rearrange("p (k c) -> p k c", k=KC), in_=wi[:, :, col:col + 512]),
                    writes=[("WVb", l, c4)], dma=True, bg=True)
            P.add("pool", lambda e, l=l, wi=wi: e.dma_start(
                out=WFb[l].rearrange("p (k c) -> p k c", k=KC), in_=wi[:, :, 6144:6152]),
                writes=[("WFb", l)], dma=True, bg=True)
            wo = w_out[l].rearrange("(k p) n -> p k n", p=128)
            for k in range(KC):
                P.add("pool", lambda e, l=l, k=k, wo=wo: e.dma_start(out=WOb[l, k], in_=wo[:, k, :]),
                      writes=[("WOb", l, k)], dma=True, bg=True)
            wu_ = w_up[l].rearrange("(k p) c -> p k c", p=128)
            for i in range(NCH):
                for (ch, col) in ((i, i * 128), (NCH + i, DFF + i * 128)):
                    P.add("pool", lambda e, l=l, ch=ch, col=col, wu_=wu_: e.dma_start(
                        out=WUPb[l, ch].rearrange("p (k c) -> p k c", k=KC), in_=wu_[:, :, col:col + 128]),
                        writes=[("WUPb", l, ch)], dma=True, bg=True)
            wd_ = w_down[l].rearrange("(k p) n -> p k n", p=128)
            for half in range(2):
                for k4 in range(NCH // 4):
                    P.add("pool", lambda e, l=l, half=half, k4=k4, wd_=wd_: e.dma_start(
                        out=WDNb[l, half, k4].rearrange("p (k n) -> p k n", k=4),
                        in_=wd_[:, 4 * k4:4 * k4 + 4, half * 1024:(half + 1) * 1024]),
                        writes=[("WDNb", l, half, k4)], dma=True, bg=True)

        def layer(l):
            Hsrc = xp if l == 0 else Hs
            Hsrc_name = "xp" if l == 0 else "Hs"
            own0 = OWN0 if l == NL - 1 else 0
            ogroups = [(g0, nb) for (g0, nb) in groups if g0 >= own0]
            t_own = own0 * 128
            P.barrier()
            cv0 = Carver()
            UT = cv0.bf(KC * T).rearrange("p (k t) -> p k t", k=KC)
            mark = cv0.off
            cv = Carver(mark)
            hbuf = [cv.f32(D) for _ in range(2)]
            gt1 = cv.f32(D)
            ubuf = [cv.bf(D) for _ in range(2)]
            stat = cv.f32(8)
            P.add("sp", lambda e, l=l: e.dma_start(out=gt1, in_=gains[l, 0]), writes=["gt1"], dma=True)
            for b in range(NB):
                s = b % 2
                P.add("sp", lambda e, b=b, s=s: e.dma_start(out=hbuf[s], in_=Hsrc[b * 128:(b + 1) * 128, :]),
                      reads=[(Hsrc_name, b)], writes=[("hbuf", s)], dma=True)
                P.add("act", lambda e, s=s: e.activation(out=ubuf[s], in_=hbuf[s], func=AF.Square,
                                                         accum_out=stat[:, 2 * s:2 * s + 1]),
                      reads=[("hbuf", s)], writes=[("ubuf", s), "p1%d_ss" % s])
                rstd_ops(stat[:, 2 * s:2 * s + 1], stat[:, 2 * s + 1:2 * s + 2], "p1%d" % s, 1.0 / D)
                P.add("dve", lambda e, s=s: e.scalar_tensor_tensor(out=ubuf[s], in0=hbuf[s], scalar=stat[:, 2 * s + 1:2 * s + 2],
                                                                   in1=gt1, op0=ALU.mult, op1=ALU.mult),
                      reads=[("hbuf", s), "p1%d_rs" % s, "gt1"], writes=[("ubuf", s)])
                for half in range(2):
                    for j in range(8):
                        k = half * 8 + j
                        P.add("pe", lambda e, s=s, k=k, j=j, half=half: e.transpose(
                            out=psb[half][:, j * 128:(j + 1) * 128], in_=ubuf[s][:, k * 128:(k + 1) * 128],
                            identity=IDENT),
                            reads=[("ubuf", s), "CB"], writes=[("psb", half)])
                    dst = UT[:, half * 8:(half + 1) * 8, b * 128:(b + 1) * 128]
                    src = psb[half][:].rearrange("p (k t) -> p k t", k=8)
                    eng = "act" if half == 0 else "dve"
                    if eng == "act":
                        P.add("act", lambda e, dst=dst, src=src: e.copy(out=dst, in_=src),
                              reads=[("psb", half)], writes=[("UT", b, half)])
                    else:
                        P.add("dve", lambda e, dst=dst, src=src: e.tensor_copy(out=dst, in_=src),
                              reads=[("psb", half)], writes=[("UT", b, half)])
            UTALL = [("UT", b) for b in range(NB)]

            wq = [cv.bf(KC * 128).rearrange("p (k c) -> p k c", k=KC) for _ in range(2)]
            stg = [cv.bf(512) for _ in range(4)]
            sgi = 0
            w_in_l = w_in[l].rearrange("(k p) c -> p k c", p=128)
            ci = 0
            for h in range(NH):
                fox = h >= 8
                hh = h - 8 if fox else h
                qcol = (3072 if fox else 0) + hh * 128
                kcol = (4096 if fox else 1024) + hh * 128
                for kind, col in (("q", qcol), ("k", kcol)):
                    s = ci % 2
                    ci += 1
                    P.add("sp", lambda e, s=s, cj=ci - 1: e.dma_start(
                        out=wq[s], in_=WQKb[l, cj].rearrange("p (k c) -> p k c", k=KC)),
                        reads=[("WQKb", l, ci - 1)], writes=[("wq", s)], dma=True)
                    for (g0, nb) in (ogroups if kind == "q" else groups):
                        n = nb * 128
                        t0 = g0 * 128
                        bk = next_bank()
                        for k in range(KC):
                            P.add("pe", lambda e, s=s, k=k, bk=bk, t0=t0, n=n: e.matmul(
                                out=psf[bk][:, 0:n], lhsT=wq[s][:, k, :], rhs=UT[:, k, t0:t0 + n],
                                start=(k == 0), stop=(k == KC - 1)),
                                reads=[("wq", s)] + [("UT", g0 + i, hf) for i in range(nb) for hf in (0, 1)], writes=[("psf", bk)])
                        g4 = sgi % 4
                        sgi += 1
                        if kind == "q":
                            P.add("act", lambda e, g4=g4, bk=bk, n=n: e.activation(
                                out=stg[g4][:, 0:n], in_=psf[bk][:, 0:n], func=AF.Copy, scale=float(HD ** -0.5)),
                                reads=[("psf", bk)], writes=[("stg", g4)])
                        else:
                            P.add("dve", lambda e, g4=g4, bk=bk, n=n: e.tensor_copy(
                                out=stg[g4][:, 0:n], in_=psf[bk][:, 0:n]),
                                reads=[("psf", bk)], writes=[("stg", g4)])
                        dstd = QTd if kind == "q" else KTd
                        P.add("sp", lambda e, g4=g4, dstd=dstd, h=h, t0=t0, n=n: e.dma_start(
                            out=dstd[h][:, t0:t0 + n], in_=stg[g4][:, 0:n]),
                            reads=[("stg", g4)], writes=[(kind + "T", h, g0)], dma=True)

            P.barrier()
            cv2 = Carver(mark)
            wv = [cv2.bf(KC * 512).rearrange("p (k c) -> p k c", k=KC) for _ in range(2)]
            vst = [cv2.bf(512) for _ in range(4)]
            vi = 0
            for c2 in range(4):
                s = c2 % 2
                P.add("sp", lambda e, s=s, c2=c2: e.dma_start(
                    out=wv[s], in_=WVb[l, c2].rearrange("p (k c) -> p k c", k=KC)),
                    reads=[("WVb", l, c2)], writes=[("wv", s)], dma=True)
                for b in range(NB):
                    bk = next_bank()
                    for k in range(KC):
                        P.add("pe", lambda e, s=s, k=k, bk=bk, b=b: e.matmul(
                            out=psf[bk][:, 0:512], lhsT=UT[:, k, b * 128:(b + 1) * 128], rhs=wv[s][:, k, :],
                            start=(k == 0), stop=(k == KC - 1)),
                            reads=[("wv", s), ("UT", b, 0), ("UT", b, 1)], writes=[("psf", bk)])
                    vs_ = vi % 4
                    vi += 1
                    if vi % 2:
                        P.add("act", lambda e, vs_=vs_, bk=bk: e.copy(out=vst[vs_], in_=psf[bk][:, 0:512]),
                              reads=[("psf", bk)], writes=[("vst", vs_)])
                    else:
                        P.add("dve", lambda e, vs_=vs_, bk=bk: e.tensor_copy(out=vst[vs_], in_=psf[bk][:, 0:512]),
                              reads=[("psf", bk)], writes=[("vst", vs_)])
                    P.add("sp", lambda e, vs_=vs_, b=b, c2=c2: e.dma_start(
                        out=Vd[b * 128:(b + 1) * 128, c2 * 512:(c2 + 1) * 512], in_=vst[vs_]),
                        reads=[("vst", vs_)], writes=[("V", b, c2)], dma=True)

            P.barrier()
            cv2 = Carver(mark)
            wf = cv2.bf(KC * 8).rearrange("p (k c) -> p k c", k=KC)
            cvk_base = ARENA_F * 4 - 4 * (6 * NB * 8 + 64) - 2 * T - 64
            cvk = Carver(cvk_base)
            NEGC = cvk.f32(NB * 8)
            NEGCB = cvk.f32(NB * 8)
            CT = cvk.bf(T)
            GH = cvk.f32(NH)
            fb = cv2.f32(NB * 8)
            bft = cv2.f32(NB * 8)
            cnb = cv2.bf(NB * 8)
            bkt = cv2.f32(NB * 8)
            P.add("sp", lambda e, l=l: e.dma_start(out=bft, in_=bfrep[l]), writes=["bft"], dma=True)
            P.add("sp", lambda e: e.dma_start(out=bkt, in_=biask), writes=["bkt"], dma=True)
            P.add("sp", lambda e, l=l: e.dma_start(out=GH, in_=ghead[l]), writes=["GH"], dma=True)
            P.add("sp", lambda e: e.dma_start(out=wf, in_=WFb[l].rearrange("p (k c) -> p k c", k=KC)),
                  reads=[("WFb", l)], writes=["wf"], dma=True)
            bkf = next_bank()
            for b in range(NB):
                for k in range(KC):
                    P.add("pe", lambda e, k=k, b=b: e.matmul(
                        out=psf[bkf][:, b * 8:(b + 1) * 8], lhsT=UT[:, k, b * 128:(b + 1) * 128], rhs=wf[:, k, :],
                        start=(k == 0), stop=(k == KC - 1)),
                        reads=["wf", ("UT", b, 0), ("UT", b, 1)], writes=[("psf", bkf)])
            P.add("dve", lambda e: e.tensor_tensor(out=fb, in0=psf[bkf][:, 0:NB * 8], in1=bft, op=ALU.add),
                  reads=[("psf", bkf), "bft"], writes=["fb"])
            P.add("act", lambda e: e.activation(out=fb, in_=fb, func=AF.Exp, scale=-1.0),
                  reads=["fb"], writes=["fb"])
            P.add("act", lambda e: e.activation(out=fb, in_=fb, func=AF.Ln, bias=ONEB),
                  reads=["fb", "ONEB"], writes=["fb"])
            bkc = next_bank()
            for b in range(NB):
                for b2 in range(b + 1):
                    P.add("pe", lambda e, b=b, b2=b2: e.matmul(
                        out=psf[bkc][:, b * 8:(b + 1) * 8], lhsT=(TRILEF if b2 == b else ONESF),
                        rhs=fb[:, b2 * 8:(b2 + 1) * 8], start=(b2 == 0), stop=(b2 == b)),
                        reads=["fb", "CF0", "CF1"], writes=[("psf", bkc)])
            P.add("dve", lambda e: e.tensor_copy(out=NEGC, in_=psf[bkc][:, 0:NB * 8]),
                  reads=[("psf", bkc)], writes=["NEGC"])
            P.add("dve", lambda e: e.tensor_tensor(out=NEGCB, in0=NEGC, in1=bkt, op=ALU.add),
                  reads=["NEGC", "bkt"], writes=["NEGCB"])
            P.add("dve", lambda e: e.tensor_scalar(out=cnb, in0=NEGC, scalar1=-1.0, scalar2=None, op0=ALU.mult),
                  reads=["NEGC"], writes=["cnb"])
            for b in range(NB):
                j = b % 8
                P.add("pe", lambda e, b=b, j=j: e.transpose(out=psb[0][0:8, j * 128:(j + 1) * 128],
                                                            in_=cnb[:, b * 8:(b + 1) * 8], identity=IDENT),
                      reads=["cnb", "CB"], writes=[("psb", 0)])
                if j == 7 or b == NB - 1:
                    bs = b - j
                    P.add("dve", lambda e, bs=bs, j=j: e.tensor_copy(out=CT[0:8, bs * 128:(bs + j + 1) * 128],
                                                                     in_=psb[0][0:8, 0:(j + 1) * 128]),
                          reads=[("psb", 0)], writes=["CT"])

            P.barrier()
            ca = Carver()
            KTs = [ca.bf(T) for _ in range(4)]
            QTs = [ca.bf(T) for _ in range(4)]
            Vs = [ca.bf(T).rearrange("p (b c) -> p b c", c=128) for _ in range(4)]
            OTst = [ca.bf(T) for _ in range(2)]
            eb = [[ca.f32(512) for _ in range(2)] for _ in range(2)]
            gb = [[ca.f32(512) for _ in range(2)] for _ in range(2)]
            spb = [[ca.bf(512) for _ in range(2)] for _ in range(2)]
            Ab = [[ca.bf(512) for _ in range(2)] for _ in range(2)]
            of32 = [ca.f32(512) for _ in range(2)]
            osq = [ca.f32(512) for _ in range(2)]
            rdb = [ca.f32(512) for _ in range(2)]
            rsb = [ca.f32(512) for _ in range(2)]
            pacc = [ca.f32(512) for _ in range(2)]
            BT = [[[ca.f32(NB) for _ in range(2)] for _ in range(2)] for _ in range(2)]
            REFS = [ca.f32(16) for _ in range(2)]
            NEGCBv = NEGCB.rearrange("p (b h) -> p b h", h=8)
            gcount = [0]
            assert ca.off <= cvk_base, (ca.off, cvk_base)
            PSV = [psf[0][:], psf[1][:], psf[2][:], psf[3][:], psf[4][:], psf[5][:],
                   psb[0][:].bitcast(F32), psb[1][:].bitcast(F32)]
            PSN = [("psf", 0), ("psf", 1), ("psf", 2), ("psf", 3), ("psf", 4), ("psf", 5), ("psb", 0), ("psb", 1)]
            SBK = [(0, 1), (4, 5)]
            XBK = [2, 6]
            OBK = [3, 7]
            VdT = Vd.rearrange("(b p) c -> p b c", p=128)
            for hp in range(NH // 2):
                heads = (2 * hp, 2 * hp + 1)
                fox = heads[0] >= 8
                ctx = []
                for j, h in enumerate(heads):
                    hs = (hp % 2) * 2 + j
                    hh = h - 8 if fox else h
                    P.add("sp", lambda e, hs=hs, h=h: e.dma_start(out=KTs[hs], in_=KTd[h]),
                          reads=[("kT", h, g0_) for (g0_, _) in groups], writes=[("KTs", hs)], dma=True)
                    P.add("sp", lambda e, hs=hs, h=h: e.dma_start(out=QTs[hs][:, t_own:T], in_=QTd[h][:, t_own:T]),
                          reads=[("qT", h, g0_) for (g0_, _) in ogroups], writes=[("QTs", hs)], dma=True)
                    P.add("sp", lambda e, hs=hs, h=h: e.dma_start(out=Vs[hs], in_=VdT[:, :, h * 128:(h + 1) * 128]),
                          reads=[("V", b, h // 4) for b in range(NB)], writes=[("Vs", hs)], dma=True)
                    ctx.append(dict(j=j, h=h, hs=hs, hh=hh, pc=0))
                for (g0, nb) in ogroups:
                    N = nb * 128
                    q0 = g0 * 128
                    kmax = g0 + nb - 1
                    order = list(range(0, kmax + 1)) if fox else list(range(kmax, -1, -1))
                    if not fox:
                        for c in ctx:
                            for bk in (XBK[c["j"]], OBK[c["j"]]):
                                P.add("pe", lambda e, bk=bk, N=N, hs=c["hs"], q0=q0: e.matmul(
                                    out=PSV[bk][:, 0:N], lhsT=ZEROB, rhs=QTs[hs][:, q0:q0 + N], start=True, stop=False),
                                    reads=[("QTs", c["hs"]), "CB"], writes=[PSN[bk]])

                    gset = gcount[0] % 2
                    gcount[0] += 1
                    if fox:
                        xb0 = XBK[0]
                        nhalf = 2 if N > 256 else 1
                        for half in range(nhalf):
                            if half == 0:
                                bref = g0 + 1 if nb >= 2 else g0
                            else:
                                bref = g0 + 3 if nb == 4 else g0 + 2
                            P.add("pe", lambda e, half=half, bref=bref: e.matmul(
                                out=PSV[xb0][:, half * 8:(half + 1) * 8], lhsT=E0F, rhs=NEGC[:, bref * 8:(bref + 1) * 8],
                                start=True, stop=True),
                                reads=["E0F", "NEGC"], writes=[PSN[xb0]])
                        P.add("act", lambda e, gset=gset, nhalf=nhalf: e.copy(out=REFS[gset][:, 0:8 * nhalf],
                                                                               in_=PSV[xb0][:, 0:8 * nhalf]),
                              reads=[PSN[xb0]], writes=[("REFS", gset)])
                        for c in ctx:
                            for half in range(nhalf):
                                P.add("dve", lambda e, gset=gset, j=c["j"], hh=c["hh"], half=half: e.tensor_scalar(
                                    out=BT[gset][j][half], in0=NEGCBv[:, :, hh],
                                    scalar1=REFS[gset][:, half * 8 + hh:half * 8 + hh + 1], scalar2=None,
                                    op0=ALU.subtract),
                                    reads=["NEGCB", ("REFS", gset)], writes=[("BT", gset, c["j"], half)])

                    def s_op(c, kb, sb, q0=q0, N=N, g0=g0, fox=fox):
                        off = max(0, kb - g0) * 128
                        n = N - off
                        diag = kb >= g0
                        hs, hh = c["hs"], c["hh"]
                        P.add("pe", lambda e: e.matmul(
                            out=PSV[sb][:, 0:n], lhsT=KTs[hs][:, kb * 128:(kb + 1) * 128],
                            rhs=QTs[hs][:, q0 + off:q0 + N], start=True, stop=(not diag)),
                            reads=[("KTs", hs), ("QTs", hs)], writes=[PSN[sb]])
                        if diag:
                            mneg = NEG_LE if fox else NEG_LT
                            P.add("pe", lambda e: e.matmul(
                                out=PSV[sb][:, 0:128], lhsT=IDENT, rhs=mneg, start=False, stop=True),
                                reads=["CB"], writes=[PSN[sb]])

                    def stage1(c, idx, kb, N=N, g0=g0, fox=fox, order=order, gset=gset):
                        j, hs, hh = c["j"], c["hs"], c["hh"]
                        sl = (c["pc"] + idx) % 2
                        sb = SBK[j][sl]
                        if idx == 0:
                            s_op(c, kb, sb)
                        if idx + 1 < len(order):
                            s_op(c, order[idx + 1], SBK[j][(c["pc"] + idx + 1) % 2])
                        off = max(0, kb - g0) * 128
                        n = N - off
                        first = idx == 0
                        last = idx == len(order) - 1
                        xb, ob = XBK[j], OBK[j]
                        if not fox:
                            P.add("act", lambda e: e.activation(out=eb[j][sl][:, 0:n], in_=PSV[sb][:, 0:n], func=AF.Exp),
                                  reads=[PSN[sb]], writes=[("eb", j, sl)])
                            P.add("act", lambda e: e.activation(out=spb[j][sl][:, 0:n], in_=eb[j][sl][:, 0:n],
                                                                func=AF.Ln, bias=ONEB),
                                  reads=[("eb", j, sl), "ONEB"], writes=[("spb", j, sl)])
                            P.add("pe", lambda e: e.matmul(out=PSV[xb][:, off:N], lhsT=NTRI_INCL, rhs=spb[j][sl][:, 0:n],
                                                           start=False, stop=False),
                                  reads=[("spb", j, sl), "CB"], writes=[PSN[xb]])
                        else:
                            for half in range(2):
                                a0 = max(off, half * 256)
                                a1 = min(N, (half + 1) * 256)
                                if a1 <= a0:
                                    continue
                                bias = BT[gset][j][half][:, kb:kb + 1]
                                P.add("act", lambda e, a0=a0, a1=a1, bias=bias: e.activation(
                                    out=Ab[j][sl][:, a0 - off:a1 - off], in_=PSV[sb][:, a0 - off:a1 - off],
                                    func=AF.Exp, bias=bias),
                                    reads=[PSN[sb], ("BT", gset, j, half)], writes=[("Ab", j, sl, half)])
                            abr = [("Ab", j, sl, 0), ("Ab", j, sl, 1)]
                            P.add("pe", lambda e: e.matmul(out=PSV[ob][:, off:N], lhsT=Vs[hs][:, kb, :],
                                                           rhs=Ab[j][sl][:, 0:n], start=first, stop=last),
                                  reads=abr + [("Vs", hs)], writes=[PSN[ob]])
                            if first:
                                P.add("dve", lambda e: e.tensor_copy(out=pacc[j][:, off:N], in_=Ab[j][sl][:, 0:n]),
                                      reads=abr, writes=[("pacc", j)])
                            else:
                                P.add("dve", lambda e: e.tensor_tensor(out=pacc[j][:, off:N], in0=pacc[j][:, off:N],
                                                                       in1=Ab[j][sl][:, 0:n], op=ALU.add),
                                      reads=abr + [("pacc", j)], writes=[("pacc", j)])

                    def stage2(c, idx, kb, N=N, g0=g0, order=order):
                        j, hs = c["j"], c["hs"]
                        sl = (c["pc"] + idx) % 2
                        off = max(0, kb - g0) * 128
                        n = N - off
                        last = idx == len(order) - 1
                        xb, ob = XBK[j], OBK[j]
                        P.add("act", lambda e: e.activation(out=gb[j][sl][:, 0:n], in_=PSV[xb][:, off:N], func=AF.Exp),
                              reads=[PSN[xb]], writes=[("gb", j, sl)])
                        P.add("pe", lambda e: e.matmul(out=PSV[xb][:, off:N], lhsT=NTRI_STRICT, rhs=spb[j][sl][:, 0:n],
                                                       start=False, stop=last),
                              reads=[("spb", j, sl), "CB"], writes=[PSN[xb]])
                        P.add("dve", lambda e: e.tensor_tensor(out=Ab[j][sl][:, 0:n], in0=eb[j][sl][:, 0:n],
                                                               in1=gb[j][sl][:, 0:n], op=ALU.mult),
                              reads=[("eb", j, sl), ("gb", j, sl)], writes=[("Ab", j, sl)])
                        P.add("pe", lambda e: e.matmul(out=PSV[ob][:, off:N], lhsT=Vs[hs][:, kb, :],
                                                       rhs=Ab[j][sl][:, 0:n], start=False, stop=last),
                              reads=[("Ab", j, sl), ("Vs", hs)], writes=[PSN[ob]])

                    for idx, kb in enumerate(order):
                        for c in ctx:
                            stage1(c, idx, kb)
                        if not fox:
                            for c in ctx:
                                stage2(c, idx, kb)
                    for c in ctx:
                        c["pc"] += len(order)

                    for c in ctx:
                        j, hs, h = c["j"], c["hs"], c["h"]
                        xb, ob = XBK[j], OBK[j]
                        ssb = SBK[j][c["pc"] % 2]
                        if fox:
                            P.add("pe", lambda e, j=j, xb=xb, N=N: e.matmul(out=PSV[xb][:, 0:N], lhsT=ONESF,
                                                                            rhs=pacc[j][:, 0:N], start=True, stop=True),
                                  reads=[("pacc", j), "CF0"], writes=[PSN[xb]])
                            P.add("dve", lambda e, j=j, xb=xb, N=N: e.tensor_scalar(
                                out=rdb[j][:, 0:N], in0=PSV[xb][:, 0:N], scalar1=1e-30, scalar2=None, op0=ALU.max),
                                reads=[PSN[xb]], writes=[("rdb", j)])
                            P.add("dve", lambda e, j=j, N=N: e.reciprocal(out=rdb[j][:, 0:N], in_=rdb[j][:, 0:N]),
                                  reads=[("rdb", j)], writes=[("rdb", j)])
                            P.add("dve", lambda e, j=j, ob=ob, N=N: e.tensor_tensor(
                                out=of32[j][:, 0:N], in0=PSV[ob][:, 0:N], in1=rdb[j][:, 0:N], op=ALU.mult),
                                reads=[PSN[ob], ("rdb", j)], writes=[("of32", j)])
                        else:
                            P.add("act", lambda e, j=j, ob=ob, N=N: e.copy(out=of32[j][:, 0:N], in_=PSV[ob][:, 0:N]),
                                  reads=[PSN[ob]], writes=[("of32", j)])
                        P.add("dve", lambda e, j=j, N=N: e.tensor_tensor(out=osq[j][:, 0:N], in0=of32[j][:, 0:N],
                                                                         in1=of32[j][:, 0:N], op=ALU.mult),
                              reads=[("of32", j)], writes=[("osq", j)])
                        P.add("pe", lambda e, j=j, ssb=ssb, N=N: e.matmul(out=PSV[ssb][:, 0:N], lhsT=ONESF, rhs=osq[j][:, 0:N],
                                                                          start=True, stop=True),
                              reads=[("osq", j), "CF0"], writes=[PSN[ssb]])
                        P.add("act", lambda e, j=j, ssb=ssb, N=N: e.activation(out=rsb[j][:, 0:N], in_=PSV[ssb][:, 0:N],
                                                                               func=AF.Ln, scale=1.0 / HD, bias=EPSB),
                              reads=[PSN[ssb], "EPSB"], writes=[("rsb", j)])
                        P.add("act", lambda e, j=j, N=N: e.activation(out=rsb[j][:, 0:N], in_=rsb[j][:, 0:N],
                                                                      func=AF.Exp, scale=-0.5),
                              reads=[("rsb", j)], writes=[("rsb", j)])
                        P.add("dve", lambda e, j=j, N=N, q0=q0, h=h: e.scalar_tensor_tensor(
                            out=OTst[j][:, q0:q0 + N], in0=of32[j][:, 0:N], scalar=GH[:, h:h + 1], in1=rsb[j][:, 0:N],
                            op0=ALU.mult, op1=ALU.mult),
                            reads=[("of32", j), ("rsb", j), "GH"], writes=[("OTst", j)])
                for c in ctx:
                    j, h = c["j"], c["h"]
                    P.add("sp", lambda e, j=j, h=h: e.dma_start(out=OTd[h][:, t_own:T], in_=OTst[j][:, t_own:T]),
                          reads=[("OTst", j)], writes=[("OT", h)], dma=True)

            P.barrier()
            c3 = Carver()
            WO = c3.bf(KC * D).rearrange("p (k n) -> p k n", k=KC)
            otb = [c3.bf(NH * 128).rearrange("p (h t) -> p h t", h=NH) for _ in range(2)]
            u2b = [c3.bf(D) for _ in range(2)]
            u2t = [c3.bf(D).rearrange("p (k t) -> p k t", k=KC) for _ in range(2)]
            junk3 = c3.bf(D)
            hb3 = [c3.f32(D) for _ in range(2)]
            yt3 = c3.f32(D)
            g3a = c3.f32(D)
            g3b = c3.f32(D)
            st3 = c3.f32(16)
            mixsb = [c3.f32(D) for _ in range(2)]
            w_out_l = w_out[l].rearrange("(k p) n -> p k n", p=128)
            for k in range(KC):
                P.add("sp", lambda e, k=k: e.dma_start(out=WO[:, k, :], in_=WOb[l, k]),
                      reads=[("WOb", l, k)], writes=[("WO", k)], dma=True)
            WOALL = [("WO", k) for k in range(KC)]
            P.add("sp", lambda e, l=l: e.dma_start(out=g3a, in_=gains[l, 1]), writes=["g3a"], dma=True)
            P.add("sp", lambda e, l=l: e.dma_start(out=g3b, in_=gains[l, 2]), writes=["g3b"], dma=True)
            OTdv = OTd.rearrange("h d t -> d h t")
            def p3A(b):
                    s = b % 2
                    P.add("sp", lambda e, s=s, b=b: e.dma_start(out=otb[s], in_=OTdv[:, :, b * 128:(b + 1) * 128]),
                          reads=[("OT", h) for h in range(NH)], writes=[("otb", s)], dma=True)
                    P.add("sp", lambda e, s=s, b=b: e.dma_start(out=hb3[s], in_=Hsrc[b * 128:(b + 1) * 128, :]),
                          reads=[(Hsrc_name, b)], writes=[("hb3", s)], dma=True)
                    banks = []
                    for c in range(4):
                        bk = next_bank()
                        banks.append(bk)
                        for h in range(NH):
                            P.add("pe", lambda e, s=s, h=h, c=c, bk=bk: e.matmul(
                                out=psf[bk][:, 0:512], lhsT=otb[s][:, h, :], rhs=WO[:, h, c * 512:(c + 1) * 512],
                                start=(h == 0), stop=(h == NH - 1)),
                                reads=[("otb", s), ("WO", h)], writes=[("psf", bk)])
                        if c % 2 == 0:
                            P.add("act", lambda e, c=c, bk=bk, s=s: e.copy(
                                out=mixsb[s][:, c * 512:(c + 1) * 512], in_=psf[bk][:, 0:512]),
                                reads=[("psf", bk)], writes=[("mixsb", s, c)])
                        else:
                            P.add("dve", lambda e, c=c, bk=bk, s=s: e.tensor_copy(
                                out=mixsb[s][:, c * 512:(c + 1) * 512], in_=psf[bk][:, 0:512]),
                                reads=[("psf", bk)], writes=[("mixsb", s, c)])
                    return banks

            def p3B(b, banks):
                    s = b % 2
                    mixall = [("mixsb", s, c) for c in range(4)]
                    P.add("act", lambda e, s=s: e.activation(out=junk3, in_=mixsb[s], func=AF.Square,
                                                             accum_out=st3[:, 8 * s + 4:8 * s + 5]),
                          reads=mixall, writes=["junk3", "p3%d_ss" % s])
                    rstd_ops(st3[:, 8 * s + 4:8 * s + 5], st3[:, 8 * s + 5:8 * s + 6], "p3%d" % s, 1.0 / D)
                    P.add("dve", lambda e, s=s: e.scalar_tensor_tensor(
                        out=yt3, in0=mixsb[s], scalar=st3[:, 8 * s + 5:8 * s + 6],
                        in1=g3a, op0=ALU.mult, op1=ALU.mult),
                        reads=mixall + ["p3%d_rs" % s, "g3a"], writes=[("yt3", c) for c in range(4)])
                    P.add("dve", lambda e, s=s: e.tensor_tensor(out=hb3[s], in0=hb3[s], in1=yt3, op=ALU.add),
                          reads=[("hb3", s)] + [("yt3", c) for c in range(4)], writes=[("hb3", s)])
                    P.add("dve", lambda e, s=s, b=b: e.tensor_scalar(out=hb3[s], in0=hb3[s], scalar1=RM[:, b:b + 1],
                                                                     scalar2=None, op0=ALU.mult),
                          reads=[("hb3", s), "RM"], writes=[("hb3", s)])
                    P.add("sp", lambda e, s=s, b=b: e.dma_start(out=Hs[b * 128:(b + 1) * 128, :], in_=hb3[s]),
                          reads=[("hb3", s)], writes=[("Hs", b)], dma=True)
                    P.add("act", lambda e, s=s: e.activation(out=u2b[s], in_=hb3[s], func=AF.Square,
                                                             accum_out=st3[:, 8 * s + 6:8 * s + 7]),
                          reads=[("hb3", s)], writes=[("u2b", s), "p3b%d_ss" % s])
                    rstd_ops(st3[:, 8 * s + 6:8 * s + 7], st3[:, 8 * s + 7:8 * s + 8], "p3b%d" % s, 1.0 / D)
                    P.add("dve", lambda e, s=s: e.scalar_tensor_tensor(out=u2b[s], in0=hb3[s], scalar=st3[:, 8 * s + 7:8 * s + 8],
                                                                       in1=g3b, op0=ALU.mult, op1=ALU.mult),
                          reads=[("hb3", s), "p3b%d_rs" % s, "g3b"], writes=[("u2b", s)])
                    for half in range(2):
                        for j in range(8):
                            k = half * 8 + j
                            P.add("pe", lambda e, s=s, k=k, j=j, half=half: e.transpose(
                                out=psb[half][:, j * 128:(j + 1) * 128], in_=u2b[s][:, k * 128:(k + 1) * 128],
                                identity=IDENT),
                                reads=[("u2b", s), "CB"], writes=[("psb", half)])
                        dst = u2t[s][:, half * 8:(half + 1) * 8, :]
                        src = psb[half][:].rearrange("p (k t) -> p k t", k=8)
                        if half == 0:
                            P.add("act", lambda e, dst=dst, src=src: e.copy(out=dst, in_=src),
                                  reads=[("psb", half)], writes=[("u2t", s, half)])
                        else:
                            P.add("dve", lambda e, dst=dst, src=src: e.tensor_copy(out=dst, in_=src),
                                  reads=[("psb", half)], writes=[("u2t", s, half)])
                    P.add("sp", lambda e, s=s, b=b: e.dma_start(out=U2Td[:, :, b * 128:(b + 1) * 128], in_=u2t[s]),
                          reads=[("u2t", s, 0), ("u2t", s, 1)], writes=[("U2T", b)], dma=True)


            blocks3 = list(range(own0, NB))
            bank_of = {blocks3[0]: p3A(blocks3[0])}
            for bi, b in enumerate(blocks3):
                if bi + 1 < len(blocks3):
                    bank_of[blocks3[bi + 1]] = p3A(blocks3[bi + 1])
                p3B(b, bank_of[b])

            P.barrier()
            c4 = Carver()
            U2 = c4.bf(KC * T).rearrange("p (k t) -> p k t", k=KC)
            wg = [c4.bf(KC * 128).rearrange("p (k c) -> p k c", k=KC) for _ in range(2)]
            wu = [c4.bf(KC * 128).rearrange("p (k c) -> p k c", k=KC) for _ in range(2)]
            gst = [c4.bf(T) for _ in range(2)]
            ag = [c4.f32(514) for _ in range(2)]
            au = [c4.f32(514) for _ in range(2)]
            yg = c4.f32(512)
            yu = c4.f32(512)
            sg = c4.f32(512)
            CP = c4.f32(2 * NCH * 4).rearrange("p (c f) -> p c f", f=4)
            for k in range(KC):
                P.add("sp", lambda e, k=k: e.dma_start(out=U2[:, k, t_own:T], in_=U2Td[:, k, t_own:T]),
                      reads=[("U2T", b) for b in range(own0, NB)], writes=[("U2", k)], dma=True)
            U2ALL = [("U2", k) for k in range(KC)]
            P.add("sp", lambda e, l=l: e.dma_start(out=CP, in_=convp[l].rearrange("p (c f) -> p c f", f=4)),
                  writes=["CP"], dma=True)
            w_up_l = w_up[l].rearrange("(k p) c -> p k c", p=128)
            for i in range(NCH):
                s = i % 2
                P.add("sp", lambda e, s=s, i=i: e.dma_start(
                    out=wg[s], in_=WUPb[l, i].rearrange("p (k c) -> p k c", k=KC)),
                    reads=[("WUPb", l, i)], writes=[("wg", s)], dma=True)
                P.add("sp", lambda e, s=s, i=i: e.dma_start(
                    out=wu[s], in_=WUPb[l, NCH + i].rearrange("p (k c) -> p k c", k=KC)),
                    reads=[("WUPb", l, NCH + i)], writes=[("wu", s)], dma=True)
                for gi, (g0, nb) in enumerate(ogroups):
                    n = nb * 128
                    t0 = g0 * 128
                    a = gi % 2
                    bg = next_bank()
                    bu = next_bank()
                    for (wt, wn, bk) in ((wg, "wg", bg), (wu, "wu", bu)):
                        for k in range(KC):
                            P.add("pe", lambda e, wt=wt, s=s, k=k, bk=bk, t0=t0, n=n: e.matmul(
                                out=psf[bk][:, 0:n], lhsT=wt[s][:, k, :], rhs=U2[:, k, t0:t0 + n],
                                start=(k == 0), stop=(k == KC - 1)),
                                reads=[(wn, s), ("U2", k)], writes=[("psf", bk)])
                    for (at, an, bk) in ((ag, "ag", bg), (au, "au", bu)):
                        if gi == 0:
                            P.add("dve", lambda e, at=at, a=a: e.memset(at[a][:, 0:2], 0.0), writes=[(an, a)])
                        else:
                            pn = ogroups[gi - 1][1] * 128
                            P.add("dve", lambda e, at=at, a=a, pn=pn: e.tensor_copy(out=at[a][:, 0:2],
                                                                                    in_=at[1 - a][:, pn:pn + 2]),
                                  reads=[(an, 1 - a)], writes=[(an, a)])
                        P.add("act", lambda e, at=at, a=a, bk=bk, n=n: e.copy(out=at[a][:, 2:2 + n], in_=psf[bk][:, 0:n]),
                              reads=[("psf", bk)], writes=[(an, a)])
                    for (at, an, yt, yn, ch) in ((ag, "ag", yg, "yg", i), (au, "au", yu, "yu", NCH + i)):
                        P.add("dve", lambda e, at=at, a=a, yt=yt, ch=ch, n=n: e.tensor_scalar(
                            out=yt[:, 0:n], in0=at[a][:, 2:2 + n], scalar1=CP[:, ch, 2:3], scalar2=CP[:, ch, 3:4],
                            op0=ALU.mult, op1=ALU.add),
                            reads=[(an, a), "CP"], writes=[yn])
                        P.add("dve", lambda e, at=at, a=a, yt=yt, ch=ch, n=n: e.scalar_tensor_tensor(
                            out=yt[:, 0:n], in0=at[a][:, 1:1 + n], scalar=CP[:, ch, 1:2], in1=yt[:, 0:n],
                            op0=ALU.mult, op1=ALU.add),
                            reads=[(an, a), "CP", yn], writes=[yn])
                        P.add("dve", lambda e, at=at, a=a, yt=yt, ch=ch, n=n: e.scalar_tensor_tensor(
                            out=yt[:, 0:n], in0=at[a][:, 0:n], scalar=CP[:, ch, 0:1], in1=yt[:, 0:n],
                            op0=ALU.mult, op1=ALU.add),
                            reads=[(an, a), "CP", yn], writes=[yn])
                    P.add("act", lambda e, n=n: e.activation(out=sg[:, 0:n], in_=yg[:, 0:n], func=AF.Silu),
                          reads=["yg"], writes=["sg"])
                    P.add("dve", lambda e, s=s, t0=t0, n=n: e.tensor_tensor(out=gst[s][:, t0:t0 + n], in0=sg[:, 0:n],
                                                                           in1=yu[:, 0:n], op=ALU.mult),
                          reads=["sg", "yu"], writes=[("gst", s)])
                P.add("sp", lambda e, s=s, i=i: e.dma_start(out=GTd[i][:, t_own:T], in_=gst[s][:, t_own:T]),
                      reads=[("gst", s)], writes=[("GT", i)], dma=True)

            P.barrier()
            c5 = Carver()
            WD = c5.bf(NCH * 1024).rearrange("p (k n) -> p k n", k=NCH)
            gtb = [c5.bf(NCH * 256).rearrange("p (k t) -> p k t", k=NCH) for _ in range(2)]
            ffs = [c5.f32(1024) for _ in range(2)]
            ffb = [c5.f32(D) for _ in range(2)]
            hb6 = [c5.f32(D) for _ in range(2)]
            g6 = c5.f32(D)
            st6 = c5.f32(8)
            junk6 = ffs[0].bitcast(BF16)
            GTdv = GTd.rearrange("i c t -> c i t")
            P.add("sp", lambda e, l=l: e.dma_start(out=g6, in_=gains[l, 3]), writes=["g6"], dma=True)
            lastl = l == NL - 1
            for half in range(2):
                for k4 in range(0, NCH, 4):
                    P.add("sp", lambda e, k4=k4, half=half: e.dma_start(
                        out=WD[:, k4:k4 + 4, :], in_=WDNb[l, half, k4 // 4].rearrange("p (k n) -> p k n", k=4)),
                        reads=[("WDNb", l, half, k4 // 4)], writes=[("WD", k4)], dma=True)
                for b in range(own0, NB):
                    s = ((b - own0) // 2) % 2
                    jb = (b - own0) % 2
                    fs = b % 2
                    if jb == 0:
                        nb2 = min(2, NB - b)
                        P.add("sp", lambda e, s=s, b=b, nb2=nb2: e.dma_start(
                            out=gtb[s][:, :, 0:nb2 * 128], in_=GTdv[:, :, b * 128:(b + nb2) * 128]),
                            reads=[("GT", i) for i in range(NCH)], writes=[("gtb", s)], dma=True)
                    if half == 1:
                        P.add("sp", lambda e, fs=fs, b=b: e.dma_start(out=ffb[fs][:, 0:1024], in_=FFd[b * 128:(b + 1) * 128, 0:1024]),
                              reads=[("FF", b, 0)], writes=[("ffb", fs, "lo")], dma=True)
                        P.add("sp", lambda e, fs=fs, b=b: e.dma_start(out=hb6[fs], in_=Hs[b * 128:(b + 1) * 128, :]),
                              reads=[("Hs", b)], writes=[("hb6", fs)], dma=True)
                    for c in range(2):
                        bk = next_bank()
                        for k in range(NCH):
                            P.add("pe", lambda e, s=s, k=k, c=c, bk=bk, jb=jb: e.matmul(
                                out=psf[bk][:, 0:512], lhsT=gtb[s][:, k, jb * 128:(jb + 1) * 128],
                                rhs=WD[:, k, c * 512:(c + 1) * 512],
                                start=(k == 0), stop=(k == NCH - 1)),
                                reads=[("gtb", s), ("WD", (k // 4) * 4)], writes=[("psf", bk)])
                        if half == 0:
                            dst, dname = ffs[fs][:, c * 512:(c + 1) * 512], ("ffs", fs, c)
                        else:
                            dst, dname = ffb[fs][:, 1024 + c * 512:1024 + (c + 1) * 512], ("ffb", fs, "hi", c)
                        if c == 0:
                            P.add("act", lambda e, dst=dst, bk=bk: e.copy(out=dst, in_=psf[bk][:, 0:512]),
                                  reads=[("psf", bk)], writes=[dname])
                        else:
                            P.add("dve", lambda e, dst=dst, bk=bk: e.tensor_copy(out=dst, in_=psf[bk][:, 0:512]),
                                  reads=[("psf", bk)], writes=[dname])
                    if half == 0:
                        P.add("sp", lambda e, fs=fs, b=b: e.dma_start(
                            out=FFd[b * 128:(b + 1) * 128, 0:1024], in_=ffs[fs]),
                            reads=[("ffs", fs, 0), ("ffs", fs, 1)], writes=[("FF", b, 0)], dma=True)
                        continue
                    ffall = [("ffb", fs, "lo"), ("ffb", fs, "hi", 0), ("ffb", fs, "hi", 1)]
                    P.add("act", lambda e, fs=fs: e.activation(out=junk6, in_=ffb[fs], func=AF.Square,
                                                               accum_out=st6[:, 2 * fs:2 * fs + 1]),
                          reads=ffall, writes=["junk6", ("ffs", 0, 0), ("ffs", 0, 1), "p6%d_ss" % fs])
                    rstd_ops(st6[:, 2 * fs:2 * fs + 1], st6[:, 2 * fs + 1:2 * fs + 2], "p6%d" % fs, 1.0 / D)
                    P.add("dve", lambda e, fs=fs: e.tensor_tensor(out=ffb[fs], in0=ffb[fs], in1=g6, op=ALU.mult),
                          reads=ffall + ["g6"], writes=ffall + [("ffb", fs, "all")])
                    P.add("dve", lambda e, fs=fs: e.scalar_tensor_tensor(out=hb6[fs], in0=ffb[fs], scalar=st6[:, 2 * fs + 1:2 * fs + 2],
                                                                         in1=hb6[fs], op0=ALU.mult, op1=ALU.add),
                          reads=ffall + [("ffb", fs, "all"), ("hb6", fs), "p6%d_rs" % fs], writes=[("hb6", fs)])
                    P.add("dve", lambda e, fs=fs, b=b: e.tensor_scalar(out=hb6[fs], in0=hb6[fs], scalar1=RM[:, b:b + 1],
                                                                       scalar2=None, op0=ALU.mult),
                          reads=[("hb6", fs), "RM"], writes=[("hb6", fs)])
                    if lastl:
                        if b >= own0 + 1:
                            i = P.add("sp", lambda e, fs=fs, b=b: e.dma_start(
                                out=y[(b - own0 - 1) * 128:(b - own0) * 128, :], in_=hb6[fs]),
                                      reads=[("hb6", fs)], writes=[("y", b)], dma=True)
                            P.final.append(i)
                    else:
                        P.add("sp", lambda e, fs=fs, b=b: e.dma_start(out=Hs[b * 128:(b + 1) * 128, :], in_=hb6[fs]),
                              reads=[("hb6", fs)], writes=[("Hs", b)], dma=True)

        for l in range(NL):
            layer(l)
        P.emit(nc, block, engsem, dmasems)
    return nc


def _consts():
    j = np.arange(128)[:, None]
    t = np.arange(128)[None, :]
    ident = (j == t).astype(np.float32)
    mask_le = (j <= t).astype(np.float32)
    mask_lt = (j < t).astype(np.float32)
    ntri_incl = -(j >= t).astype(np.float32)
    ntri_strict = -(j < t).astype(np.float32)
    ones = np.ones((128, 128), np.float32)
    tri_le = (j <= t).astype(np.float32)
    zeros = np.zeros((128, 128), np.float32)
    neg_lt = np.where(j < t, 0.0, -30000.0).astype(np.float32)
    neg_le = np.where(j <= t, 0.0, -30000.0).astype(np.float32)
    e0 = np.zeros((128, 128), np.float32)
    e0[0, :] = 1.0
    cm = np.concatenate([ident, mask_le, mask_lt, ntri_incl, ntri_strict, ones, tri_le, zeros, neg_lt, neg_le, e0], axis=1)
    sel = np.zeros((8, 8, 128), np.float32)
    for h in range(8):
        sel[h, h, :] = 1.0
    return cm, sel.reshape(8, 1024)


_PROG_CACHE = {}


def _run(inputs, NB, batches, NL=2):
    x = np.asarray(inputs["x"], np.float32)
    meta = np.asarray(inputs["meta"], np.float32)
    T = NB * 128
    nreal = (NB - 1) * 128
    cm, sel = _consts()
    cm_dev = cm.copy()
    gains = np.stack([np.stack([np.broadcast_to(np.asarray(inputs[k], np.float32)[l][None, :], (128, D))
                                for k in ("g_mix_pre", "g_mix_post", "g_ffn_pre", "g_ffn_post")])
                      for l in range(NL)]).astype(np.float32)
    ghead = np.stack([np.concatenate([np.asarray(inputs["g_sb"], np.float32)[l],
                                      np.asarray(inputs["g_fox"], np.float32)[l]], axis=0).T
                      for l in range(NL)]).astype(np.float32)
    bfrep = np.stack([np.broadcast_to(np.tile(np.asarray(inputs["b_f"], np.float32)[l], NB)[None, :], (128, NB * 8))
                      for l in range(NL)]).astype(np.float32)
    cw = np.asarray(inputs["conv_w"], np.float32)
    cb = np.asarray(inputs["conv_b"], np.float32)
    convp = np.zeros((NL, 128, 2 * NCH, 4), np.float32)
    for l in range(NL):
        for k in range(3):
            convp[l, :, :, k] = cw[l, k].reshape(2 * NCH, 128).T
        convp[l, :, :, 3] = cb[l].reshape(2 * NCH, 128).T
    convp = convp.reshape(NL, 128, 2 * NCH * 4)
    common = dict(
        w_in=np.ascontiguousarray(np.asarray(inputs["w_in"], np.float32)[:NL]),
        w_out=np.ascontiguousarray(np.asarray(inputs["w_out"], np.float32)[:NL]),
        w_up=np.ascontiguousarray(np.asarray(inputs["w_up"], np.float32)[:NL]),
        w_down=np.ascontiguousarray(np.asarray(inputs["w_down"], np.float32)[:NL]),
        gains=gains, ghead=ghead, bfrep=bfrep, convp=convp, cmat=cm_dev, selm=sel)
    own0 = (NB - 1) // 2
    in_maps = []
    for c in range(8):
        b = batches[c]
        role = c % 2
        xpad = np.zeros((T, D), np.float32)
        rm = np.ones((128, NB), np.float32)
        if role == 1:
            xpad[NPADROWS:128] = meta
            xpad[128:] = x[b, :nreal]
            rm[:NPADROWS, 0] = 0.0
        else:
            o = own0 * 128
            xpad[o + NPADROWS:o + 128] = meta
            xpad[o + 128:] = x[b, :nreal - o]
            rm[:, :own0] = 0.0
            rm[:NPADROWS, own0] = 0.0
        bk = np.repeat(np.where(rm == 0.0, -30000.0, 0.0).astype(np.float32), 8, axis=1)
        m = dict(common)
        m["xp"] = xpad
        m["rowm"] = rm
        m["biask"] = np.ascontiguousarray(bk)
        in_maps.append(m)
    key = (NB, NL)
    if key not in _PROG_CACHE:
        _PROG_CACHE[key] = build_program(NB, NL)
    nc = _PROG_CACHE[key]
    res = run_bass_kernel_spmd(nc, in_maps, core_ids=list(range(8)))
    return res


def kernel(x, meta, g_mix_pre, w_in, b_f, g_sb, g_fox, w_out, g_mix_post, g_ffn_pre, w_up, conv_w,
           conv_b, w_down, g_ffn_post):
    inputs = dict(x=x, meta=meta, g_mix_pre=g_mix_pre, w_in=w_in, b_f=b_f, g_sb=g_sb, g_fox=g_fox,
                  w_out=w_out, g_mix_post=g_mix_post, g_ffn_pre=g_ffn_pre, w_up=w_up, conv_w=conv_w,
                  conv_b=conv_b, w_down=w_down, g_ffn_post=g_ffn_post)
    batches = [c // 2 for c in range(8)]
    res = _run(inputs, 33, batches)
    out = np.stack([np.concatenate([np.asarray(res.results[2 * b]["y"], np.float32),
                                    np.asarray(res.results[2 * b + 1]["y"], np.float32)], axis=0)
                    for b in range(4)], axis=0)
    return out
```

```python
import numpy as np
import ml_dtypes
import concourse.bass as bass
import concourse.mybir as mybir
from concourse.bass_utils import run_bass_kernel_spmd

F32 = mybir.dt.float32
BF16 = mybir.dt.bfloat16
AF = mybir.ActivationFunctionType
ALU = mybir.AluOpType

D = 2048
KC = 16
NH = 16
HD = 128
NIN = 6152
DFF = 5632
NCH = 44
EPS = 1e-6
NMETA = 16
NPADROWS = 112
SAME_ENGINE_SYNC = True
NDMASEM = 20


class Prog:
    ENGS = ("pe", "act", "dve", "pool", "sp")

    def __init__(self):
        self.ops = []
        self.last_w = {}
        self.readers = {}
        self.final = []
        self.pending = {e: set() for e in self.ENGS}
        self.dma_since = []
        self.last_compute = {}

    def barrier(self):
        deps = set(self.last_compute.values()) | set(self.dma_since)
        for e in self.ENGS:
            self.pending[e] |= deps
        self.dma_since = []

    def add(self, eng, fn, reads=(), writes=(), dma=False, bg=False):
        i = len(self.ops)
        deps = set(self.pending[eng])
        self.pending[eng] = set()
        for b in reads:
            w = self.last_w.get(b)
            if w is not None:
                deps.add(w)
        for b in writes:
            w = self.last_w.get(b)
            if w is not None:
                deps.add(w)
            deps.update(self.readers.get(b, ()))
        keep = set()
        for d in deps:
            p = self.ops[d]
            if (not p["dma"]) and p["eng"] == eng:
                if eng == "pe" or not SAME_ENGINE_SYNC:
                    continue
            keep.add(d)
        self.ops.append(dict(eng=eng, fn=fn, dma=dma, deps=keep))
        if dma:
            if not bg:
                self.dma_since.append(i)
        else:
            self.last_compute[eng] = i
        for b in reads:
            self.readers.setdefault(b, []).append(i)
        for b in writes:
            self.last_w[b] = i
            self.readers[b] = []
        return i

    def emit(self, nc, block, engsem, dmasems):
        ops = self.ops
        dcount = {"sp": 0, "pool": 0}
        hist = {"sp": [], "pool": []}
        for i, op in enumerate(ops):
            if op["dma"]:
                q = op["eng"]
                k = dcount[q]
                dcount[q] += 1
                op["sem"] = dmasems[q][k % NDMASEM]
                op["val"] = 16 * (k // NDMASEM + 1)
                if k >= NDMASEM:
                    op["deps"].add(hist[q][k - NDMASEM])
                hist[q].append(i)
        flagged = set()
        for op in ops:
            for d in op["deps"]:
                if not ops[d]["dma"]:
                    flagged.add(d)
        cnt = {e: 0 for e in self.ENGS}
        for i, op in enumerate(ops):
            if (not op["dma"]) and i in flagged:
                cnt[op["eng"]] += 1
                op["sem"] = engsem[op["eng"]]
                op["val"] = cnt[op["eng"]]
        streams = {e: [] for e in self.ENGS}
        for i, op in enumerate(ops):
            streams[op["eng"]].append(i)
        final = self.final

        def run(engname, e):
            waited = {}
            for i in streams[engname]:
                op = ops[i]
                need = {}
                for d in op["deps"]:
                    p = ops[d]
                    s, v = p["sem"], p["val"]
                    key = id(s)
                    if key not in need or need[key][1] < v:
                        need[key] = (s, v)
                for key, (s, v) in need.items():
                    if waited.get(key, 0) >= v:
                        continue
                    e.wait_ge(s, v)
                    waited[key] = v
                ins = op["fn"](e)
                if op["dma"]:
                    ins.then_inc(op["sem"], 16)
                elif i in flagged:
                    ins.then_inc(op["sem"], 1)
            if engname == "sp":
                for d in final:
                    p = ops[d]
                    e.wait_ge(p["sem"], p["val"])

        @block.tensor
        def _(e):
            run("pe", e)

        @block.scalar
        def _(e):
            run("act", e)

        @block.vector
        def _(e):
            run("dve", e)

        @block.gpsimd
        def _(e):
            run("pool", e)

        @block.sync
        def _(e):
            run("sp", e)


def build_program(NB, NL=2):
    T = NB * 128
    nc = bass.Bass("TRN2", target_bir_lowering=False)
    P = Prog()

    def din(name, shape):
        return nc.dram_tensor(name, list(shape), F32, kind="ExternalInput").ap()

    xp = din("xp", [T, D])
    w_in = din("w_in", [NL, D, NIN])
    w_out = din("w_out", [NL, D, D])
    w_up = din("w_up", [NL, D, 2 * DFF])
    w_down = din("w_down", [NL, DFF, D])
    gains = din("gains", [NL, 4, 128, D])
    ghead = din("ghead", [NL, 128, NH])
    bfrep = din("bfrep", [NL, 128, NB * 8])
    convp = din("convp", [NL, 128, 2 * NCH * 4])
    cmat = din("cmat", [128, 11 * 128])
    selm = din("selm", [8, 8 * 128])
    rowm = din("rowm", [128, NB])
    biask = din("biask", [128, NB * 8])
    OWN0 = (NB - 1) // 2
    assert OWN0 % 4 == 0
    NOUT = NB - 1 - OWN0
    y = nc.dram_tensor("y", [NOUT * 128, D], F32, kind="ExternalOutput").ap()

    Hs = nc.dram_tensor("Hs", [T, D], F32).ap()
    QTd = nc.dram_tensor("QTd", [NH, 128, T], BF16).ap()
    KTd = nc.dram_tensor("KTd", [NH, 128, T], BF16).ap()
    Vd = nc.dram_tensor("Vd", [T, D], BF16).ap()
    OTd = nc.dram_tensor("OTd", [NH, 128, T], BF16).ap()
    U2Td = nc.dram_tensor("U2Td", [128, KC, T], BF16).ap()
    GTd = nc.dram_tensor("GTd", [NCH, 128, T], BF16).ap()
    FFd = nc.dram_tensor("FFd", [T, D], F32).ap()
    WQKb = nc.dram_tensor("WQKb", [NL, 32, 128, KC * 128], BF16).ap()
    WVb = nc.dram_tensor("WVb", [NL, 4, 128, KC * 512], BF16).ap()
    WFb = nc.dram_tensor("WFb", [NL, 128, KC * 8], BF16).ap()
    WOb = nc.dram_tensor("WOb", [NL, KC, 128, D], BF16).ap()
    WUPb = nc.dram_tensor("WUPb", [NL, 2 * NCH, 128, KC * 128], BF16).ap()
    WDNb = nc.dram_tensor("WDNb", [NL, 2, NCH // 4, 128, 4 * 1024], BF16).ap()

    ARENA_F = 46 * 1024
    CONST_F = 2304

    from contextlib import ExitStack
    with ExitStack() as es:
        arena_t = es.enter_context(nc.sbuf_tensor("arena", [128, ARENA_F], F32))
        const_t = es.enter_context(nc.sbuf_tensor("consts", [128, CONST_F], F32))
        psf = [es.enter_context(nc.psum_tensor(f"psf{i}", [128, 512], F32)) for i in range(6)]
        psb = [es.enter_context(nc.psum_tensor(f"psb{i}", [128, 1024], BF16)) for i in range(2)]
        engsem = {e: es.enter_context(nc.semaphore(f"sem_{e}")) for e in Prog.ENGS}
        dmasems = {q: [es.enter_context(nc.semaphore(f"dsem_{q}{i}")) for i in range(NDMASEM)]
                   for q in ("sp", "pool")}
        block = es.enter_context(nc.Block())

        arena_f = arena_t[:]
        arena_b = arena_t[:].bitcast(BF16)
        const_f = const_t[:]
        const_b = const_t[:].bitcast(BF16)

        class Carver:
            def __init__(self, base=0):
                self.off = base

            def f32(self, n):
                o = (self.off + 3) // 4
                self.off = (o + n) * 4
                assert self.off <= ARENA_F * 4, self.off
                return arena_f[:, o:o + n]

            def bf(self, n):
                o = (self.off + 1) // 2
                self.off = (o + n) * 2
                assert self.off <= ARENA_F * 4, self.off
                return arena_b[:, o:o + n]

        CB = const_b[:, 0:1280]
        SELB = const_b[0:8, 1280:2304]
        CFo = 1280
        CF = const_f[:, CFo:CFo + 256]
        MF = const_f[:, CFo + 256:CFo + 512]
        RM = const_f[:, CFo + 530:CFo + 530 + NB]
        E0F = const_f[:, CFo + 600:CFo + 728]
        P.add("sp", lambda e: e.dma_start(out=E0F, in_=cmat[:, 1280:1408]), writes=["E0F"], dma=True)
        IDENT = CB[:, 0:128]
        MASK_LE = CB[:, 128:256]
        NTRI_INCL = CB[:, 384:512]
        NTRI_STRICT = CB[:, 512:640]
        ONESB = CB[:, 640:768]
        ZEROB = CB[:, 896:1024]
        NEG_LT = CB[:, 1024:1152]
        NEG_LE = CB[:, 1152:1280]
        ONESF = CF[:, 0:128]
        TRILEF = CF[:, 128:256]
        MASK_LT_F = MF[:, 0:128]

        P.add("pool", lambda e: e.dma_start(out=CB, in_=cmat[:, 0:1280]), writes=["CB"], dma=True)
        P.add("pool", lambda e: e.dma_start(out=SELB, in_=selm), writes=["SELB"], dma=True)
        P.add("sp", lambda e: e.dma_start(out=CF[:, 0:128], in_=cmat[:, 640:768]), writes=["CF0"], dma=True)
        P.add("sp", lambda e: e.dma_start(out=CF[:, 128:256], in_=cmat[:, 768:896]), writes=["CF1"], dma=True)
        P.add("sp", lambda e: e.dma_start(out=MF[:, 0:128], in_=cmat[:, 256:384]), writes=["MF"], dma=True)
        P.add("sp", lambda e: e.dma_start(out=RM, in_=rowm), writes=["RM"], dma=True)

        ring = {"i": 0}

        def next_bank():
            b = ring["i"] % 6
            ring["i"] += 1
            return b

        groups = []
        b0 = 0
        while b0 < NB:
            nb = min(4, NB - b0)
            groups.append((b0, nb))
            b0 += nb

        def rstd_ops(ss, rs, tag, mul):
            P.add("act", lambda e: e.activation(out=rs, in_=ss, func=AF.Ln, scale=mul, bias=EPSB),
                  reads=[tag + "_ss", "EPSB"], writes=[tag + "_rs"])
            P.add("act", lambda e: e.activation(out=rs, in_=rs, func=AF.Exp, scale=-0.5),
                  reads=[tag + "_rs"], writes=[tag + "_rs"])

        EPSB = const_f[:, CFo + 520:CFo + 521]
        ONEB = const_f[:, CFo + 521:CFo + 522]
        P.add("dve", lambda e: e.memset(EPSB, EPS), writes=["EPSB"])
        P.add("dve", lambda e: e.memset(ONEB, 1.0), writes=["ONEB"])

        def qk_cols(h):
            fox_ = h >= 8
            hh_ = h - 8 if fox_ else h
            return (3072 if fox_ else 0) + hh_ * 128, (4096 if fox_ else 1024) + hh_ * 128

        for l in range(NL):
            wi = w_in[l].rearrange("(k p) c -> p k c", p=128)
            ci = 0
            for h in range(NH):
                for col in qk_cols(h):
                    P.add("pool", lambda e, l=l, ci=ci, col=col, wi=wi: e.dma_start(
                        out=WQKb[l, ci].rearrange("p (k c) -> p k c", k=KC), in_=wi[:, :, col:col + 128]),
                        writes=[("WQKb", l, ci)], dma=True, bg=True)
                    ci += 1
            for c4 in range(4):
                col = (2048 if c4 < 2 else 5120) + (c4 % 2) * 512
                P.add("pool", lambda e, l=l, c4=c4, col=col, wi=wi: e.dma_start(
                    out=WVb[l, c4].rearrange("p (k c) -> p k c", k=KC), in_=wi[:, :, col:col + 512]),
                    writes=[("WVb", l, c4)], dma=True, bg=True)
            P.add("pool", lambda e, l=l, wi=wi: e.dma_start(
                out=WFb[l].rearrange("p (k c) -> p k c", k=KC), in_=wi[:, :, 6144:6152]),
                writes=[("WFb", l)], dma=True, bg=True)
            wo = w_out[l].rearrange("(k p) n -> p k n", p=128)
            for k in range(KC):
                P.add("pool", lambda e, l=l, k=k, wo=wo: e.dma_start(out=WOb[l, k], in_=wo[:, k, :]),
                      writes=[("WOb", l, k)], dma=True, bg=True)
            wu_ = w_up[l].rearrange("(k p) c -> p k c", p=128)
            for i in range(NCH):
                for (ch, col) in ((i, i * 128), (NCH + i, DFF + i * 128)):
                    P.add("pool", lambda e, l=l, ch=ch, col=col, wu_=wu_: e.dma_start(
                        out=WUPb[l, ch].rearrange("p (k c) -> p k c", k=KC), in_=wu_[:, :, col:col + 128]),
                        writes=[("WUPb", l, ch)], dma=True, bg=True)
            wd_ = w_down[l].rearrange("(k p) n -> p k n", p=128)
            for half in range(2):
                for k4 in range(NCH // 4):
                    P.add("pool", lambda e, l=l, half=half, k4=k4, wd_=wd_: e.dma_start(
                        out=WDNb[l, half, k4].rearrange("p (k n) -> p k n", k=4),
                        in_=wd_[:, 4 * k4:4 * k4 + 4, half * 1024:(half + 1) * 1024]),
                        writes=[("WDNb", l, half, k4)], dma=True, bg=True)

        def layer(l):
            Hsrc = xp if l == 0 else Hs
            Hsrc_name = "xp" if l == 0 else "Hs"
            own0 = OWN0 if l == NL - 1 else 0
            ogroups = [(g0, nb) for (g0, nb) in groups if g0 >= own0]
            t_own = own0 * 128
            P.barrier()
            cv0 = Carver()
            UT = cv0.bf(KC * T).rearrange("p (k t) -> p k t", k=KC)
            mark = cv0.off
            cv = Carver(mark)
            hbuf = [cv.f32(D) for _ in range(2)]
            gt1 = cv.f32(D)
            ubuf = [cv.bf(D) for _ in range(2)]
            stat = cv.f32(8)
            P.add("sp", lambda e, l=l: e.dma_start(out=gt1, in_=gains[l, 0]), writes=["gt1"], dma=True)
            for b in range(NB):
                s = b % 2
                P.add("sp", lambda e, b=b, s=s: e.dma_start(out=hbuf[s], in_=Hsrc[b * 128:(b + 1) * 128, :]),
                      reads=[(Hsrc_name, b)], writes=[("hbuf", s)], dma=True)
                P.add("act", lambda e, s=s: e.activation(out=ubuf[s], in_=hbuf[s], func=AF.Square,
                                                         accum_out=stat[:, 2 * s:2 * s + 1]),
                      reads=[("hbuf", s)], writes=[("ubuf", s), "p1%d_ss" % s])
                rstd_ops(stat[:, 2 * s:2 * s + 1], stat[:, 2 * s + 1:2 * s + 2], "p1%d" % s, 1.0 / D)
                P.add("dve", lambda e, s=s: e.scalar_tensor_tensor(out=ubuf[s], in0=hbuf[s], scalar=stat[:, 2 * s + 1:2 * s + 2],
                                                                   in1=gt1, op0=ALU.mult, op1=ALU.mult),
                      reads=[("hbuf", s), "p1%d_rs" % s, "gt1"], writes=[("ubuf", s)])
                for half in range(2):
                    for j in range(8):
                        k = half * 8 + j
                        P.add("pe", lambda e, s=s, k=k, j=j, half=half: e.transpose(
                            out=psb[half][:, j * 128:(j + 1) * 128], in_=ubuf[s][:, k * 128:(k + 1) * 128],
                            identity=IDENT),
                            reads=[("ubuf", s), "CB"], writes=[("psb", half)])
                    dst = UT[:, half * 8:(half + 1) * 8, b * 128:(b + 1) * 128]
                    src = psb[half][:].rearrange("p (k t) -> p k t", k=8)
                    eng = "act" if half == 0 else "dve"
                    if eng == "act":
                        P.add("act", lambda e, dst=dst, src=src: e.copy(out=dst, in_=src),
                              reads=[("psb", half)], writes=[("UT", b, half)])
                    else:
                        P.add("dve", lambda e, dst=dst, src=src: e.tensor_copy(out=dst, in_=src),
                              reads=[("psb", half)], writes=[("UT", b, half)])
            UTALL = [("UT", b) for b in range(NB)]

            wq = [cv.bf(KC * 128).rearrange("p (k c) -> p k c", k=KC) for _ in range(2)]
            stg = [cv.bf(512) for _ in range(4)]
            sgi = 0
            w_in_l = w_in[l].rearrange("(k p) c -> p k c", p=128)
            ci = 0
            for h in range(NH):
                fox = h >= 8
                hh = h - 8 if fox else h
                qcol = (3072 if fox else 0) + hh * 128
                kcol = (4096 if fox else 1024) + hh * 128
                for kind, col in (("q", qcol), ("k", kcol)):
                    s = ci % 2
                    ci += 1
                    P.add("sp", lambda e, s=s, cj=ci - 1: e.dma_start(
                        out=wq[s], in_=WQKb[l, cj].rearrange("p (k c) -> p k c", k=KC)),
                        reads=[("WQKb", l, ci - 1)], writes=[("wq", s)], dma=True)
                    for (g0, nb) in (ogroups if kind == "q" else groups):
                        n = nb * 128
                        t0 = g0 * 128
                        bk = next_bank()
                        for k in range(KC):
                            P.add("pe", lambda e, s=s, k=k, bk=bk, t0=t0, n=n: e.matmul(
                                out=psf[bk][:, 0:n], lhsT=wq[s][:, k, :], rhs=UT[:, k, t0:t0 + n],
                                start=(k == 0), stop=(k == KC - 1)),
                                reads=[("wq", s)] + [("UT", g0 + i, hf) for i in range(nb) for hf in (0, 1)], writes=[("psf", bk)])
                        g4 = sgi % 4
                        sgi += 1
                        if kind == "q":
                            P.add("act", lambda e, g4=g4, bk=bk, n=n: e.activation(
                                out=stg[g4][:, 0:n], in_=psf[bk][:, 0:n], func=AF.Copy, scale=float(HD ** -0.5)),
                                reads=[("psf", bk)], writes=[("stg", g4)])
                        else:
                            P.add("dve", lambda e, g4=g4, bk=bk, n=n: e.tensor_copy(
                                out=stg[g4][:, 0:n], in_=psf[bk][:, 0:n]),
                                reads=[("psf", bk)], writes=[("stg", g4)])
                        dstd = QTd if kind == "q" else KTd
                        P.add("sp", lambda e, g4=g4, dstd=dstd, h=h, t0=t0, n=n: e.dma_start(
                            out=dstd[h][:, t0:t0 + n], in_=stg[g4][:, 0:n]),
                            reads=[("stg", g4)], writes=[(kind + "T", h, g0)], dma=True)

            P.barrier()
            cv2 = Carver(mark)
            wv = [cv2.bf(KC * 512).rearrange("p (k c) -> p k c", k=KC) for _ in range(2)]
            vst = [cv2.bf(512) for _ in range(4)]
            vi = 0
            for c2 in range(4):
                s = c2 % 2
                P.add("sp", lambda e, s=s, c2=c2: e.dma_start(
                    out=wv[s], in_=WVb[l, c2].rearrange("p (k c) -> p k c", k=KC)),
                    reads=[("WVb", l, c2)], writes=[("wv", s)], dma=True)
                for b in range(NB):
                    bk = next_bank()
                    for k in range(KC):
                        P.add("pe", lambda e, s=s, k=k, bk=bk, b=b: e.matmul(
                            out=psf[bk][:, 0:512], lhsT=UT[:, k, b * 128:(b + 1) * 128], rhs=wv[s][:, k, :],
                            start=(k == 0), stop=(k == KC - 1)),
                            reads=[("wv", s), ("UT", b, 0), ("UT", b, 1)], writes=[("psf", bk)])
                    vs_ = vi % 4
                    vi += 1
                    if vi % 2:
                        P.add("act", lambda e, vs_=vs_, bk=bk: e.copy(out=vst[vs_], in_=psf[bk][:, 0:512]),
                              reads=[("psf", bk)], writes=[("vst", vs_)])
                    else:
                        P.add("dve", lambda e, vs_=vs_, bk=bk: e.tensor_copy(out=vst[vs_], in_=psf[bk][:, 0:512]),
                              reads=[("psf", bk)], writes=[("vst", vs_)])
                    P.add("sp", lambda e, vs_=vs_, b=b, c2=c2: e.dma_start(
                        out=Vd[b * 128:(b + 1) * 128, c2 * 512:(c2 + 1) * 512], in_=vst[vs_]),
                        reads=[("vst", vs_)], writes=[("V", b, c2)], dma=True)

            P.barrier()
            cv2 = Carver(mark)
            wf = cv2.bf(KC * 8).rearrange("p (k c) -> p k c", k=KC)
            cvk_base = ARENA_F * 4 - 4 * (6 * NB * 8 + 64) - 2 * T - 64
            cvk = Carver(cvk_base)
            NEGC = cvk.f32(NB * 8)
            NEGCB = cvk.f32(NB * 8)
            CT = cvk.bf(T)
            GH = cvk.f32(NH)
            fb = cv2.f32(NB * 8)
            bft = cv2.f32(NB * 8)
            cnb = cv2.bf(NB * 8)
            bkt = cv2.f32(NB * 8)
            P.add("sp", lambda e, l=l: e.dma_start(out=bft, in_=bfrep[l]), writes=["bft"], dma=True)
            P.add("sp", lambda e: e.dma_start(out=bkt, in_=biask), writes=["bkt"], dma=True)
            P.add("sp", lambda e, l=l: e.dma_start(out=GH, in_=ghead[l]), writes=["GH"], dma=True)
            P.add("sp", lambda e: e.dma_start(out=wf, in_=WFb[l].rearrange("p (k c) -> p k c", k=KC)),
                  reads=[("WFb", l)], writes=["wf"], dma=True)
            bkf = next_bank()
            for b in range(NB):
                for k in range(KC):
                    P.add("pe", lambda e, k=k, b=b: e.matmul(
                        out=psf[bkf][:, b * 8:(b + 1) * 8], lhsT=UT[:, k, b * 128:(b + 1) * 128], rhs=wf[:, k, :],
                        start=(k == 0), stop=(k == KC - 1)),
                        reads=["wf", ("UT", b, 0), ("UT", b, 1)], writes=[("psf", bkf)])
            P.add("dve", lambda e: e.tensor_tensor(out=fb, in0=psf[bkf][:, 0:NB * 8], in1=bft, op=ALU.add),
                  reads=[("psf", bkf), "bft"], writes=["fb"])
            P.add("act", lambda e: e.activation(out=fb, in_=fb, func=AF.Exp, scale=-1.0),
                  reads=["fb"], writes=["fb"])
            P.add("act", lambda e: e.activation(out=fb, in_=fb, func=AF.Ln, bias=ONEB),
                  reads=["fb", "ONEB"], writes=["fb"])
            bkc = next_bank()
            for b in range(NB):
                for b2 in range(b + 1):
                    P.add("pe", lambda e, b=b, b2=b2: e.matmul(
                        out=psf[bkc][:, b * 8:(b + 1) * 8], lhsT=(TRILEF if b2 == b else ONESF),
                        rhs=fb[:, b2 * 8:(b2 + 1) * 8], start=(b2 == 0), stop=(b2 == b)),
                        reads=["fb", "CF0", "CF1"], writes=[("psf", bkc)])
            P.add("dve", lambda e: e.tensor_copy(out=NEGC, in_=psf[bkc][:, 0:NB * 8]),
                  reads=[("psf", bkc)], writes=["NEGC"])
            P.add("dve", lambda e: e.tensor_tensor(out=NEGCB, in0=NEGC, in1=bkt, op=ALU.add),
                  reads=["NEGC", "bkt"], writes=["NEGCB"])
            P.add("dve", lambda e: e.tensor_scalar(out=cnb, in0=NEGC, scalar1=-1.0, scalar2=None, op0=ALU.mult),
                  reads=["NEGC"], writes=["cnb"])
            for b in range(NB):
                j = b % 8
                P.add("pe", lambda e, b=b, j=j: e.transpose(out=psb[0][0:8, j * 128:(j + 1) * 128],
                                                            in_=cnb[:, b * 8:(b + 1) * 8], identity=IDENT),
                      reads=["cnb", "CB"], writes=[("psb", 0)])
                if j == 7 or b == NB - 1:
                    bs = b - j
                    P.add("dve", lambda e, bs=bs, j=j: e.tensor_copy(out=CT[0:8, bs * 128:(bs + j + 1) * 128],
                                                                     in_=psb[0][0:8, 0:(j + 1) * 128]),
                          reads=[("psb", 0)], writes=["CT"])

            P.barrier()
            ca = Carver()
            KTs = [ca.bf(T) for _ in range(4)]
            QTs = [ca.bf(T) for _ in range(4)]
            Vs = [ca.bf(T).rearrange("p (b c) -> p b c", c=128) for _ in range(4)]
            OTst = [ca.bf(T) for _ in range(2)]
            eb = [[ca.f32(512) for _ in range(2)] for _ in range(2)]
            gb = [[ca.f32(512) for _ in range(2)] for _ in range(2)]
            spb = [[ca.bf(512) for _ in range(2)] for _ in range(2)]
            Ab = [[ca.bf(512) for _ in range(2)] for _ in range(2)]
            of32 = [ca.f32(512) for _ in range(2)]
            osq = [ca.f32(512) for _ in range(2)]
            rdb = [ca.f32(512) for _ in range(2)]
            rsb = [ca.f32(512) for _ in range(2)]
            pacc = [ca.f32(512) for _ in range(2)]
            BT = [[[ca.f32(NB) for _ in range(2)] for _ in range(2)] for _ in range(2)]
            REFS = [ca.f32(16) for _ in range(2)]
            NEGCBv = NEGCB.rearrange("p (b h) -> p b h", h=8)
            gcount = [0]
            assert ca.off <= cvk_base, (ca.off, cvk_base)
            PSV = [psf[0][:], psf[1][:], psf[2][:], psf[3][:], psf[4][:], psf[5][:],
                   psb[0][:].bitcast(F32), psb[1][:].bitcast(F32)]
            PSN = [("psf", 0), ("psf", 1), ("psf", 2), ("psf", 3), ("psf", 4), ("psf", 5), ("psb", 0), ("psb", 1)]
            SBK = [(0, 1), (4, 5)]
            XBK = [2, 6]
            OBK = [3, 7]
            VdT = Vd.rearrange("(b p) c -> p b c", p=128)
            for hp in range(NH // 2):
                heads = (2 * hp, 2 * hp + 1)
                fox = heads[0] >= 8
                ctx = []
                for j, h in enumerate(heads):
                    hs = (hp % 2) * 2 + j
                    hh = h - 8 if fox else h
                    P.add("sp", lambda e, hs=hs, h=h: e.dma_start(out=KTs[hs], in_=KTd[h]),
                          reads=[("kT", h, g0_) for (g0_, _) in groups], writes=[("KTs", hs)], dma=True)
                    P.add("sp", lambda e, hs=hs, h=h: e.dma_start(out=QTs[hs][:, t_own:T], in_=QTd[h][:, t_own:T]),
                          reads=[("qT", h, g0_) for (g0_, _) in ogroups], writes=[("QTs", hs)], dma=True)
                    P.add("sp", lambda e, hs=hs, h=h: e.dma_start(out=Vs[hs], in_=VdT[:, :, h * 128:(h + 1) * 128]),
                          reads=[("V", b, h // 4) for b in range(NB)], writes=[("Vs", hs)], dma=True)
                    ctx.append(dict(j=j, h=h, hs=hs, hh=hh, pc=0))
                for (g0, nb) in ogroups:
                    N = nb * 128
                    q0 = g0 * 128
                    kmax = g0 + nb - 1
                    order = list(range(0, kmax + 1)) if fox else list(range(kmax, -1, -1))
                    if not fox:
                        for c in ctx:
                            for bk in (XBK[c["j"]], OBK[c["j"]]):
                                P.add("pe", lambda e, bk=bk, N=N, hs=c["hs"], q0=q0: e.matmul(
                                    out=PSV[bk][:, 0:N], lhsT=ZEROB, rhs=QTs[hs][:, q0:q0 + N], start=True, stop=False),
                                    reads=[("QTs", c["hs"]), "CB"], writes=[PSN[bk]])

                    gset = gcount[0] % 2
                    gcount[0] += 1
                    if fox:
                        xb0 = XBK[0]
                        nhalf = 2 if N > 256 else 1
                        for half in range(nhalf):
                            if half == 0:
                                bref = g0 + 1 if nb >= 2 else g0
                            else:
                                bref = g0 + 3 if nb == 4 else g0 + 2
                            P.add("pe", lambda e, half=half, bref=bref: e.matmul(
                                out=PSV[xb0][:, half * 8:(half + 1) * 8], lhsT=E0F, rhs=NEGC[:, bref * 8:(bref + 1) * 8],
                                start=True, stop=True),
                                reads=["E0F", "NEGC"], writes=[PSN[xb0]])
                        P.add("act", lambda e, gset=gset, nhalf=nhalf: e.copy(out=REFS[gset][:, 0:8 * nhalf],
                                                                               in_=PSV[xb0][:, 0:8 * nhalf]),
                              reads=[PSN[xb0]], writes=[("REFS", gset)])
                        for c in ctx:
                            for half in range(nhalf):
                                P.add("dve", lambda e, gset=gset, j=c["j"], hh=c["hh"], half=half: e.tensor_scalar(
                                    out=BT[gset][j][half], in0=NEGCBv[:, :, hh],
                                    scalar1=REFS[gset][:, half * 8 + hh:half * 8 + hh + 1], scalar2=None,
                                    op0=ALU.subtract),
                                    reads=["NEGCB", ("REFS", gset)], writes=[("BT", gset, c["j"], half)])

                    def s_op(c, kb, sb, q0=q0, N=N, g0=g0, fox=fox):
                        off = max(0, kb - g0) * 128
                        n = N - off
                        diag = kb >= g0
                        hs, hh = c["hs"], c["hh"]
                        P.add("pe", lambda e: e.matmul(
                            out=PSV[sb][:, 0:n], lhsT=KTs[hs][:, kb * 128:(kb + 1) * 128],
                            rhs=QTs[hs][:, q0 + off:q0 + N], start=True, stop=(not diag)),
                            reads=[("KTs", hs), ("QTs", hs)], writes=[PSN[sb]])
                        if diag:
                            mneg = NEG_LE if fox else NEG_LT
                            P.add("pe", lambda e: e.matmul(
                                out=PSV[sb][:, 0:128], lhsT=IDENT, rhs=mneg, start=False, stop=True),
                                reads=["CB"], writes=[PSN[sb]])

                    def stage1(c, idx, kb, N=N, g0=g0, fox=fox, order=order, gset=gset):
                        j, hs, hh = c["j"], c["hs"], c["hh"]
                        sl = (c["pc"] + idx) % 2
                        sb = SBK[j][sl]
                        if idx == 0:
                            s_op(c, kb, sb)
                        if idx + 1 < len(order):
                            s_op(c, order[idx + 1], SBK[j][(c["pc"] + idx + 1) % 2])
                        off = max(0, kb - g0) * 128
                        n = N - off
                        first = idx == 0
                        last = idx == len(order) - 1
                        xb, ob = XBK[j], OBK[j]
                        if not fox:
                            P.add("act", lambda e: e.activation(out=eb[j][sl][:, 0:n], in_=PSV[sb][:, 0:n], func=AF.Exp),
                                  reads=[PSN[sb]], writes=[("eb", j, sl)])
                            P.add("act", lambda e: e.activation(out=spb[j][sl][:, 0:n], in_=eb[j][sl][:, 0:n],
                                                                func=AF.Ln, bias=ONEB),
                                  reads=[("eb", j, sl), "ONEB"], writes=[("spb", j, sl)])
                            P.add("pe", lambda e: e.matmul(out=PSV[xb][:, off:N], lhsT=NTRI_INCL, rhs=spb[j][sl][:, 0:n],
                                                           start=False, stop=False),
                                  reads=[("spb", j, sl), "CB"], writes=[PSN[xb]])
                        else:
                            for half in range(2):
                                a0 = max(off, half * 256)
                                a1 = min(N, (half + 1) * 256)
                                if a1 <= a0:
                                    continue
                                bias = BT[gset][j][half][:, kb:kb + 1]
                                P.add("act", lambda e, a0=a0, a1=a1, bias=bias: e.activation(
                                    out=Ab[j][sl][:, a0 - off:a1 - off], in_=PSV[sb][:, a0 - off:a1 - off],
                                    func=AF.Exp, bias=bias),
                                    reads=[PSN[sb], ("BT", gset, j, half)], writes=[("Ab", j, sl, half)])
                            abr = [("Ab", j, sl, 0), ("Ab", j, sl, 1)]
                            P.add("pe", lambda e: e.matmul(out=PSV[ob][:, off:N], lhsT=Vs[hs][:, kb, :],
                                                           rhs=Ab[j][sl][:, 0:n], start=first, stop=last),
                                  reads=abr + [("Vs", hs)], writes=[PSN[ob]])
                            if first:
                                P.add("dve", lambda e: e.tensor_copy(out=pacc[j][:, off:N], in_=Ab[j][sl][:, 0:n]),
                                      reads=abr, writes=[("pacc", j)])
                            else:
                                P.add("dve", lambda e: e.tensor_tensor(out=pacc[j][:, off:N], in0=pacc[j][:, off:N],
                                                                       in1=Ab[j][sl][:, 0:n], op=ALU.add),
                                      reads=abr + [("pacc", j)], writes=[("pacc", j)])

                    def stage2(c, idx, kb, N=N, g0=g0, order=order):
                        j, hs = c["j"], c["hs"]
                        sl = (c["pc"] + idx) % 2
                        off = max(0, kb - g0) * 128
                        n = N - off
                        last = idx == len(order) - 1
                        xb, ob = XBK[j], OBK[j]
                        P.add("act", lambda e: e.activation(out=gb[j][sl][:, 0:n], in_=PSV[xb][:, off:N], func=AF.Exp),
                              reads=[PSN[xb]], writes=[("gb", j, sl)])
                        P.add("pe", lambda e: e.matmul(out=PSV[xb][:, off:N], lhsT=NTRI_STRICT, rhs=spb[j][sl][:, 0:n],
                                                       start=False, stop=last),
                              reads=[("spb", j, sl), "CB"], writes=[PSN[xb]])
                        P.add("dve", lambda e: e.tensor_tensor(out=Ab[j][sl][:, 0:n], in0=eb[j][sl][:, 0:n],
                                                               in1=gb[j][sl][:, 0:n], op=ALU.mult),
                              reads=[("eb", j, sl), ("gb", j, sl)], writes=[("Ab", j, sl)])
                        P.add("pe", lambda e: e.matmul(out=PSV[ob][:, off:N], lhsT=Vs[hs][:, kb, :],
                                                       rhs=Ab[j][sl][:, 0:n], start=False, stop=last),
                              reads=[("Ab", j, sl), ("Vs", hs)], writes=[PSN[ob]])

                    for idx, kb in enumerate(order):
                        for c in ctx:
                            stage1(c, idx, kb)
                        if not fox:
                            for c in ctx:
                                stage2(c, idx, kb)
                    for c in ctx:
                        c["pc"] += len(order)

                    for c in ctx:
                        j, hs, h = c["j"], c["hs"], c["h"]
                        xb, ob = XBK[j], OBK[j]
                        ssb = SBK[j][c["pc"] % 2]
                        if fox:
                            P.add("pe", lambda e, j=j, xb=xb, N=N: e.matmul(out=PSV[xb][:, 0:N], lhsT=ONESF,
                                                                            rhs=pacc[j][:, 0:N], start=True, stop=True),
                                  reads=[("pacc", j), "CF0"], writes=[PSN[xb]])
                            P.add("dve", lambda e, j=j, xb=xb, N=N: e.tensor_scalar(
                                out=rdb[j][:, 0:N], in0=PSV[xb][:, 0:N], scalar1=1e-30, scalar2=None, op0=ALU.max),
                                reads=[PSN[xb]], writes=[("rdb", j)])
                            P.add("dve", lambda e, j=j, N=N: e.reciprocal(out=rdb[j][:, 0:N], in_=rdb[j][:, 0:N]),
                                  reads=[("rdb", j)], writes=[("rdb", j)])
                            P.add("dve", lambda e, j=j, ob=ob, N=N: e.tensor_tensor(
                                out=of32[j][:, 0:N], in0=PSV[ob][:, 0:N], in1=rdb[j][:, 0:N], op=ALU.mult),
                                reads=[PSN[ob], ("rdb", j)], writes=[("of32", j)])
                        else:
                            P.add("act", lambda e, j=j, ob=ob, N=N: e.copy(out=of32[j][:, 0:N], in_=PSV[ob][:, 0:N]),
                                  reads=[PSN[ob]], writes=[("of32", j)])
                        P.add("dve", lambda e, j=j, N=N: e.tensor_tensor(out=osq[j][:, 0:N], in0=of32[j][:, 0:N],
                                                                         in1=of32[j][:, 0:N], op=ALU.mult),
                              reads=[("of32", j)], writes=[("osq", j)])
                        P.add("pe", lambda e, j=j, ssb=ssb, N=N: e.matmul(out=PSV[ssb][:, 0:N], lhsT=ONESF, rhs=osq[j][:, 0:N],
                                                                          start=True, stop=True),
                              reads=[("osq", j), "CF0"], writes=[PSN[ssb]])
                        P.add("act", lambda e, j=j, ssb=ssb, N=N: e.activation(out=rsb[j][:, 0:N], in_=PSV[ssb][:, 0:N],
                                                                               func=AF.Ln, scale=1.0 / HD, bias=EPSB),
                              reads=[PSN[ssb], "EPSB"], writes=[("rsb", j)])
                        P.add("act", lambda e, j=j, N=N: e.activation(out=rsb[j][:, 0:N], in_=rsb[j][:, 0:N],
                                                                      func=AF.Exp, scale=-0.5),
                              reads=[("rsb", j)], writes=[("rsb", j)])
                        P.add("dve", lambda e, j=j, N=N, q0=q0, h=h: e.scalar_tensor_tensor(
                            out=OTst[j][:, q0:q0 + N], in0=of32[j][:, 0:N], scalar=GH[:, h:h + 1], in1=rsb[j][:, 0:N],
                            op0=ALU.mult, op1=ALU.mult),
                            reads=[("of32", j), ("rsb", j), "GH"], writes=[("OTst", j)])
                for c in ctx:
                    j, h = c["j"], c["h"]
                    P.add("sp", lambda e, j=j, h=h: e.dma_start(out=OTd[h][:, t_own:T], in_=OTst[j][:, t_own:T]),
                          reads=[("OTst", j)], writes=[("OT", h)], dma=True)

            P.barrier()
            c3 = Carver()
            WO = c3.bf(KC * D).rearrange("p (k n) -> p k n", k=KC)
            otb = [c3.bf(NH * 128).rearrange("p (h t) -> p h t", h=NH) for _ in range(2)]
            u2b = [c3.bf(D) for _ in range(2)]
            u2t = [c3.bf(D).rearrange("p (k t) -> p k t", k=KC) for _ in range(2)]
            junk3 = c3.bf(D)
            hb3 = [c3.f32(D) for _ in range(2)]
            yt3 = c3.f32(D)
            g3a = c3.f32(D)
            g3b = c3.f32(D)
            st3 = c3.f32(16)
            mixsb = [c3.f32(D) for _ in range(2)]
            w_out_l = w_out[l].rearrange("(k p) n -> p k n", p=128)
            for k in range(KC):
                P.add("sp", lambda e, k=k: e.dma_start(out=WO[:, k, :], in_=WOb[l, k]),
                      reads=[("WOb", l, k)], writes=[("WO", k)], dma=True)
            WOALL = [("WO", k) for k in range(KC)]
            P.add("sp", lambda e, l=l: e.dma_start(out=g3a, in_=gains[l, 1]), writes=["g3a"], dma=True)
            P.add("sp", lambda e, l=l: e.dma_start(out=g3b, in_=gains[l, 2]), writes=["g3b"], dma=True)
            OTdv = OTd.rearrange("h d t -> d h t")
            def p3A(b):
                    s = b % 2
                    P.add("sp", lambda e, s=s, b=b: e.dma_start(out=otb[s], in_=OTdv[:, :, b * 128:(b + 1) * 128]),
                          reads=[("OT", h) for h in range(NH)], writes=[("otb", s)], dma=True)
                    P.add("sp", lambda e, s=s, b=b: e.dma_start(out=hb3[s], in_=Hsrc[b * 128:(b + 1) * 128, :]),
                          reads=[(Hsrc_name, b)], writes=[("hb3", s)], dma=True)
                    banks = []
                    for c in range(4):
                        bk = next_bank()
                        banks.append(bk)
                        for h in range(NH):
                            P.add("pe", lambda e, s=s, h=h, c=c, bk=bk: e.matmul(
                                out=psf[bk][:, 0:512], lhsT=otb[s][:, h, :], rhs=WO[:, h, c * 512:(c + 1) * 512],
                                start=(h == 0), stop=(h == NH - 1)),
                                reads=[("otb", s), ("WO", h)], writes=[("psf", bk)])
                        if c % 2 == 0:
                            P.add("act", lambda e, c=c, bk=bk, s=s: e.copy(
                                out=mixsb[s][:, c * 512:(c + 1) * 512], in_=psf[bk][:, 0:512]),
                                reads=[("psf", bk)], writes=[("mixsb", s, c)])
                        else:
                            P.add("dve", lambda e, c=c, bk=bk, s=s: e.tensor_copy(
                                out=mixsb[s][:, c * 512:(c + 1) * 512], in_=psf[bk][:, 0:512]),
                                reads=[("psf", bk)], writes=[("mixsb", s, c)])
                    return banks

            def p3B(b, banks):
                    s = b % 2
                    mixall = [("mixsb", s, c) for c in range(4)]
                    P.add("act", lambda e, s=s: e.activation(out=junk3, in_=mixsb[s], func=AF.Square,
                                                             accum_out=st3[:, 8 * s + 4:8 * s + 5]),
                          reads=mixall, writes=["junk3", "p3%d_ss" % s])
                    rstd_ops(st3[:, 8 * s + 4:8 * s + 5], st3[:, 8 * s + 5:8 * s + 6], "p3%d" % s, 1.0 / D)
                    P.add("dve", lambda e, s=s: e.scalar_tensor_tensor(
                        out=yt3, in0=mixsb[s], scalar=st3[:, 8 * s + 5:8 * s + 6],
                        in1=g3a, op0=ALU.mult, op1=ALU.mult),
                        reads=mixall + ["p3%d_rs" % s, "g3a"], writes=[("yt3", c) for c in range(4)])
                    P.add("dve", lambda e, s=s: e.tensor_tensor(out=hb3[s], in0=hb3[s], in1=yt3, op=ALU.add),
                          reads=[("hb3", s)] + [("yt3", c) for c in range(4)], writes=[("hb3", s)])
                    P.add("dve", lambda e, s=s, b=b: e.tensor_scalar(out=hb3[s], in0=hb3[s], scalar1=RM[:, b:b + 1],
                                                                     scalar2=None, op0=ALU.mult),
                          reads=[("hb3", s), "RM"], writes=[("hb3", s)])
                    P.add("sp", lambda e, s=s, b=b: e.dma_start(out=Hs[b * 128:(b + 1) * 128, :], in_=hb3[s]),
                          reads=[("hb3", s)], writes=[("Hs", b)], dma=True)
                    P.add("act", lambda e, s=s: e.activation(out=u2b[s], in_=hb3[s], func=AF.Square,
                                                             accum_out=st3[:, 8 * s + 6:8 * s + 7]),
                          reads=[("hb3", s)], writes=[("u2b", s), "p3b%d_ss" % s])
                    rstd_ops(st3[:, 8 * s + 6:8 * s + 7], st3[:, 8 * s + 7:8 * s + 8], "p3b%d" % s, 1.0 / D)
                    P.add("dve", lambda e, s=s: e.scalar_tensor_tensor(out=u2b[s], in0=hb3[s], scalar=st3[:, 8 * s + 7:8 * s + 8],
                                                                       in1=g3b, op0=ALU.mult, op1=ALU.mult),
                          reads=[("hb3", s), "p3b%d_rs" % s, "g3b"], writes=[("u2b", s)])
                    for half in range(2):
                        for j in range(8):
                            k = half * 8 + j
                            P.add("pe", lambda e, s=s, k=k, j=j, half=half: e.transpose(
                                out=psb[half][:, j * 128:(j + 1) * 128], in_=u2b[s][:, k * 128:(k + 1) * 128],
                                identity=IDENT),
                                reads=[("u2b", s), "CB"], writes=[("psb", half)])
                        dst = u2t[s][:, half * 8:(half + 1) * 8, :]
                        src = psb[half][:].rearrange("p (k t) -> p k t", k=8)
                        if half == 0:
                            P.add("act", lambda e, dst=dst, src=src: e.copy(out=dst, in_=src),
                                  reads=[("psb", half)], writes=[("u2t", s, half)])
                        else:
                            P.add("dve", lambda e, dst=dst, src=src: e.tensor_copy(out=dst, in_=src),
                                  reads=[("psb", half)], writes=[("u2t", s, half)])
                    P.add("sp", lambda e, s=s, b=b: e.dma_start(out=U2Td[:, :, b * 128:(b + 1) * 128], in_=u2t[s]),
                          reads=[("u2t", s, 0), ("u2t", s, 1)], writes=[("U2T", b)], dma=True)


            blocks3 = list(range(own0, NB))
            bank_of = {blocks3[0]: p3A(blocks3[0])}
            for bi, b in enumerate(blocks3):
                if bi + 1 < len(blocks3):
                    bank_of[blocks3[bi + 1]] = p3A(blocks3[bi + 1])
                p3B(b, bank_of[b])

            P.barrier()
            c4 = Carver()
            U2 = c4.bf(KC * T).rearrange("p (k t) -> p k t", k=KC)
            wg = [c4.bf(KC * 128).rearrange("p (k c) -> p k c", k=KC) for _ in range(2)]
            wu = [c4.bf(KC * 128).rearrange("p (k c) -> p k c", k=KC) for _ in range(2)]
            gst = [c4.bf(T) for _ in range(2)]
            ag = [c4.f32(514) for _ in range(2)]
            au = [c4.f32(514) for _ in range(2)]
            yg = c4.f32(512)
            yu = c4.f32(512)
            sg = c4.f32(512)
            CP = c4.f32(2 * NCH * 4).rearrange("p (c f) -> p c f", f=4)
            for k in range(KC):
                P.add("sp", lambda e, k=k: e.dma_start(out=U2[:, k, t_own:T], in_=U2Td[:, k, t_own:T]),
                      reads=[("U2T", b) for b in range(own0, NB)], writes=[("U2", k)], dma=True)
            U2ALL = [("U2", k) for k in range(KC)]
            P.add("sp", lambda e, l=l: e.dma_start(out=CP, in_=convp[l].rearrange("p (c f) -> p c f", f=4)),
                  writes=["CP"], dma=True)
            w_up_l = w_up[l].rearrange("(k p) c -> p k c", p=128)
            for i in range(NCH):
                s = i % 2
                for i2 in ([0, 1] if i == 0 else [i + 1]):
                    if i2 >= NCH:
                        continue
                    s2 = i2 % 2
                    P.add("sp", lambda e, s2=s2, i2=i2: e.dma_start(
                        out=wg[s2], in_=WUPb[l, i2].rearrange("p (k c) -> p k c", k=KC)),
                        reads=[("WUPb", l, i2)], writes=[("wg", s2)], dma=True)
                    P.add("sp", lambda e, s2=s2, i2=i2: e.dma_start(
                        out=wu[s2], in_=WUPb[l, NCH + i2].rearrange("p (k c) -> p k c", k=KC)),
                        reads=[("WUPb", l, NCH + i2)], writes=[("wu", s2)], dma=True)
                for gi, (g0, nb) in enumerate(ogroups):
                    n = nb * 128
                    t0 = g0 * 128
                    a = gi % 2
                    bg = next_bank()
                    bu = next_bank()
                    for (wt, wn, bk) in ((wg, "wg", bg), (wu, "wu", bu)):
                        for k in range(KC):
                            P.add("pe", lambda e, wt=wt, s=s, k=k, bk=bk, t0=t0, n=n: e.matmul(
                                out=psf[bk][:, 0:n], lhsT=wt[s][:, k, :], rhs=U2[:, k, t0:t0 + n],
                                start=(k == 0), stop=(k == KC - 1)),
                                reads=[(wn, s), ("U2", k)], writes=[("psf", bk)])
                    for (at, an, bk) in ((ag, "ag", bg), (au, "au", bu)):
                        if gi == 0:
                            P.add("dve", lambda e, at=at, a=a: e.memset(at[a][:, 0:2], 0.0), writes=[(an, a)])
                        else:
                            pn = ogroups[gi - 1][1] * 128
                            P.add("dve", lambda e, at=at, a=a, pn=pn: e.tensor_copy(out=at[a][:, 0:2],
                                                                                    in_=at[1 - a][:, pn:pn + 2]),
                                  reads=[(an, 1 - a)], writes=[(an, a)])
                        P.add("act", lambda e, at=at, a=a, bk=bk, n=n: e.copy(out=at[a][:, 2:2 + n], in_=psf[bk][:, 0:n]),
                              reads=[("psf", bk)], writes=[(an, a)])
                    for (at, an, yt, yn, ch) in ((ag, "ag", yg, "yg", i), (au, "au", yu, "yu", NCH + i)):
                        P.add("dve", lambda e, at=at, a=a, yt=yt, ch=ch, n=n: e.tensor_scalar(
                            out=yt[:, 0:n], in0=at[a][:, 2:2 + n], scalar1=CP[:, ch, 2:3], scalar2=CP[:, ch, 3:4],
                            op0=ALU.mult, op1=ALU.add),
                            reads=[(an, a), "CP"], writes=[yn])
                        P.add("dve", lambda e, at=at, a=a, yt=yt, ch=ch, n=n: e.scalar_tensor_tensor(
                            out=yt[:, 0:n], in0=at[a][:, 1:1 + n], scalar=CP[:, ch, 1:2], in1=yt[:, 0:n],
                            op0=ALU.mult, op1=ALU.add),
                            reads=[(an, a), "CP", yn], writes=[yn])
                        P.add("dve", lambda e, at=at, a=a, yt=yt, ch=ch, n=n: e.scalar_tensor_tensor(
                            out=yt[:, 0:n], in0=at[a][:, 0:n], scalar=CP[:, ch, 0:1], in1=yt[:, 0:n],
                            op0=ALU.mult, op1=ALU.add),
                            reads=[(an, a), "CP", yn], writes=[yn])
                    P.add("act", lambda e, n=n: e.activation(out=sg[:, 0:n], in_=yg[:, 0:n], func=AF.Silu),
                          reads=["yg"], writes=["sg"])
                    P.add("dve", lambda e, s=s, t0=t0, n=n: e.tensor_tensor(out=gst[s][:, t0:t0 + n], in0=sg[:, 0:n],
                                                                           in1=yu[:, 0:n], op=ALU.mult),
                          reads=["sg", "yu"], writes=[("gst", s)])
                P.add("sp", lambda e, s=s, i=i: e.dma_start(out=GTd[i][:, t_own:T], in_=gst[s][:, t_own:T]),
                      reads=[("gst", s)], writes=[("GT", i)], dma=True)

            P.barrier()
            c5 = Carver()
            WD = c5.bf(NCH * 1024).rearrange("p (k n) -> p k n", k=NCH)
            gtb = [c5.bf(NCH * 256).rearrange("p (k t) -> p k t", k=NCH) for _ in range(2)]
            ffs = [c5.f32(1024) for _ in range(2)]
            ffb = [c5.f32(D) for _ in range(2)]
            hb6 = [c5.f32(D) for _ in range(2)]
            g6 = c5.f32(D)
            st6 = c5.f32(8)
            junk6 = ffs[0].bitcast(BF16)
            GTdv = GTd.rearrange("i c t -> c i t")
            P.add("sp", lambda e, l=l: e.dma_start(out=g6, in_=gains[l, 3]), writes=["g6"], dma=True)
            lastl = l == NL - 1
            for half in range(2):
                for k4 in range(0, NCH, 4):
                    P.add("sp", lambda e, k4=k4, half=half: e.dma_start(
                        out=WD[:, k4:k4 + 4, :], in_=WDNb[l, half, k4 // 4].rearrange("p (k n) -> p k n", k=4)),
                        reads=[("WDNb", l, half, k4 // 4)], writes=[("WD", k4)], dma=True)
                for b in range(own0, NB):
                    s = ((b - own0) // 2) % 2
                    jb = (b - own0) % 2
                    fs = b % 2
                    if jb == 0:
                        for bb in ([b, b + 2] if b == own0 else [b + 2]):
                            if bb >= NB:
                                continue
                            sx = ((bb - own0) // 2) % 2
                            nb2 = min(2, NB - bb)
                            P.add("sp", lambda e, sx=sx, bb=bb, nb2=nb2: e.dma_start(
                                out=gtb[sx][:, :, 0:nb2 * 128], in_=GTdv[:, :, bb * 128:(bb + nb2) * 128]),
                                reads=[("GT", i) for i in range(NCH)], writes=[("gtb", sx)], dma=True)
                    if half == 1:
                        P.add("sp", lambda e, fs=fs, b=b: e.dma_start(out=ffb[fs][:, 0:1024], in_=FFd[b * 128:(b + 1) * 128, 0:1024]),
                              reads=[("FF", b, 0)], writes=[("ffb", fs, "lo")], dma=True)
                        P.add("sp", lambda e, fs=fs, b=b: e.dma_start(out=hb6[fs], in_=Hs[b * 128:(b + 1) * 128, :]),
                              reads=[("Hs", b)], writes=[("hb6", fs)], dma=True)
                    for c in range(2):
                        bk = next_bank()
                        for k in range(NCH):
                            P.add("pe", lambda e, s=s, k=k, c=c, bk=bk, jb=jb: e.matmul(
                                out=psf[bk][:, 0:512], lhsT=gtb[s][:, k, jb * 128:(jb + 1) * 128],
                                rhs=WD[:, k, c * 512:(c + 1) * 512],
                                start=(k == 0), stop=(k == NCH - 1)),
                                reads=[("gtb", s), ("WD", (k // 4) * 4)], writes=[("psf", bk)])
                        if half == 0:
                            dst, dname = ffs[fs][:, c * 512:(c + 1) * 512], ("ffs", fs, c)
                        else:
                            dst, dname = ffb[fs][:, 1024 + c * 512:1024 + (c + 1) * 512], ("ffb", fs, "hi", c)
                        if c == 0:
                            P.add("act", lambda e, dst=dst, bk=bk: e.copy(out=dst, in_=psf[bk][:, 0:512]),
                                  reads=[("psf", bk)], writes=[dname])
                        else:
                            P.add("dve", lambda e, dst=dst, bk=bk: e.tensor_copy(out=dst, in_=psf[bk][:, 0:512]),
                                  reads=[("psf", bk)], writes=[dname])
                    if half == 0:
                        P.add("sp", lambda e, fs=fs, b=b: e.dma_start(
                            out=FFd[b * 128:(b + 1) * 128, 0:1024], in_=ffs[fs]),
                            reads=[("ffs", fs, 0), ("ffs", fs, 1)], writes=[("FF", b, 0)], dma=True)
                        continue
                    ffall = [("ffb", fs, "lo"), ("ffb", fs, "hi", 0), ("ffb", fs, "hi", 1)]
                    P.add("act", lambda e, fs=fs: e.activation(out=junk6, in_=ffb[fs], func=AF.Square,
                                                               accum_out=st6[:, 2 * fs:2 * fs + 1]),
                          reads=ffall, writes=["junk6", ("ffs", 0, 0), ("ffs", 0, 1), "p6%d_ss" % fs])
                    rstd_ops(st6[:, 2 * fs:2 * fs + 1], st6[:, 2 * fs + 1:2 * fs + 2], "p6%d" % fs, 1.0 / D)
                    P.add("dve", lambda e, fs=fs: e.tensor_tensor(out=ffb[fs], in0=ffb[fs], in1=g6, op=ALU.mult),
                          reads=ffall + ["g6"], writes=ffall + [("ffb", fs, "all")])
                    P.add("dve", lambda e, fs=fs: e.scalar_tensor_tensor(out=hb6[fs], in0=ffb[fs], scalar=st6[:, 2 * fs + 1:2 * fs + 2],
                                                                         in1=hb6[fs], op0=ALU.mult, op1=ALU.add),
                          reads=ffall + [("ffb", fs, "all"), ("hb6", fs), "p6%d_rs" % fs], writes=[("hb6", fs)])
                    P.add("dve", lambda e, fs=fs, b=b: e.tensor_scalar(out=hb6[fs], in0=hb6[fs], scalar1=RM[:, b:b + 1],
                                                                       scalar2=None, op0=ALU.mult),
                          reads=[("hb6", fs), "RM"], writes=[("hb6", fs)])
                    if lastl:
                        if b >= own0 + 1:
                            i = P.add("sp", lambda e, fs=fs, b=b: e.dma_start(
                                out=y[(b - own0 - 1) * 128:(b - own0) * 128, :], in_=hb6[fs]),
                                      reads=[("hb6", fs)], writes=[("y", b)], dma=True)
                            P.final.append(i)
                    else:
                        P.add("sp", lambda e, fs=fs, b=b: e.dma_start(out=Hs[b * 128:(b + 1) * 128, :], in_=hb6[fs]),
                              reads=[("hb6", fs)], writes=[("Hs", b)], dma=True)

        for l in range(NL):
            layer(l)
        P.emit(nc, block, engsem, dmasems)
    return nc


def _consts():
    j = np.arange(128)[:, None]
    t = np.arange(128)[None, :]
    ident = (j == t).astype(np.float32)
    mask_le = (j <= t).astype(np.float32)
    mask_lt = (j < t).astype(np.float32)
    ntri_incl = -(j >= t).astype(np.float32)
    ntri_strict = -(j < t).astype(np.float32)
    ones = np.ones((128, 128), np.float32)
    tri_le = (j <= t).astype(np.float32)
    zeros = np.zeros((128, 128), np.float32)
    neg_lt = np.where(j < t, 0.0, -30000.0).astype(np.float32)
    neg_le = np.where(j <= t, 0.0, -30000.0).astype(np.float32)
    e0 = np.zeros((128, 128), np.float32)
    e0[0, :] = 1.0
    cm = np.concatenate([ident, mask_le, mask_lt, ntri_incl, ntri_strict, ones, tri_le, zeros, neg_lt, neg_le, e0], axis=1)
    sel = np.zeros((8, 8, 128), np.float32)
    for h in range(8):
        sel[h, h, :] = 1.0
    return cm, sel.reshape(8, 1024)


_PROG_CACHE = {}


def _run(inputs, NB, batches, NL=2):
    x = np.asarray(inputs["x"], np.float32)
    meta = np.asarray(inputs["meta"], np.float32)
    T = NB * 128
    nreal = (NB - 1) * 128
    cm, sel = _consts()
    cm_dev = cm.copy()
    gains = np.stack([np.stack([np.broadcast_to(np.asarray(inputs[k], np.float32)[l][None, :], (128, D))
                                for k in ("g_mix_pre", "g_mix_post", "g_ffn_pre", "g_ffn_post")])
                      for l in range(NL)]).astype(np.float32)
    ghead = np.stack([np.concatenate([np.asarray(inputs["g_sb"], np.float32)[l],
                                      np.asarray(inputs["g_fox"], np.float32)[l]], axis=0).T
                      for l in range(NL)]).astype(np.float32)
    bfrep = np.stack([np.broadcast_to(np.tile(np.asarray(inputs["b_f"], np.float32)[l], NB)[None, :], (128, NB * 8))
                      for l in range(NL)]).astype(np.float32)
    cw = np.asarray(inputs["conv_w"], np.float32)
    cb = np.asarray(inputs["conv_b"], np.float32)
    convp = np.zeros((NL, 128, 2 * NCH, 4), np.float32)
    for l in range(NL):
        for k in range(3):
            convp[l, :, :, k] = cw[l, k].reshape(2 * NCH, 128).T
        convp[l, :, :, 3] = cb[l].reshape(2 * NCH, 128).T
    convp = convp.reshape(NL, 128, 2 * NCH * 4)
    common = dict(
        w_in=np.ascontiguousarray(np.asarray(inputs["w_in"], np.float32)[:NL]),
        w_out=np.ascontiguousarray(np.asarray(inputs["w_out"], np.float32)[:NL]),
        w_up=np.ascontiguousarray(np.asarray(inputs["w_up"], np.float32)[:NL]),
        w_down=np.ascontiguousarray(np.asarray(inputs["w_down"], np.float32)[:NL]),
        gains=gains, ghead=ghead, bfrep=bfrep, convp=convp, cmat=cm_dev, selm=sel)
    own0 = (NB - 1) // 2
    in_maps = []
    for c in range(8):
        b = batches[c]
        role = c % 2
        xpad = np.zeros((T, D), np.float32)
        rm = np.ones((128, NB), np.float32)
        if role == 1:
            xpad[NPADROWS:128] = meta
            xpad[128:] = x[b, :nreal]
            rm[:NPADROWS, 0] = 0.0
        else:
            o = own0 * 128
            xpad[o + NPADROWS:o + 128] = meta
            xpad[o + 128:] = x[b, :nreal - o]
            rm[:, :own0] = 0.0
            rm[:NPADROWS, own0] = 0.0
        bk = np.repeat(np.where(rm == 0.0, -30000.0, 0.0).astype(np.float32), 8, axis=1)
        m = dict(common)
        m["xp"] = xpad
        m["rowm"] = rm
        m["biask"] = np.ascontiguousarray(bk)
        in_maps.append(m)
    key = (NB, NL)
    if key not in _PROG_CACHE:
        _PROG_CACHE[key] = build_program(NB, NL)
    nc = _PROG_CACHE[key]
    res = run_bass_kernel_spmd(nc, in_maps, core_ids=list(range(8)))
    return res


def kernel(x, meta, g_mix_pre, w_in, b_f, g_sb, g_fox, w_out, g_mix_post, g_ffn_pre, w_up, conv_w,
           conv_b, w_down, g_ffn_post):
    inputs = dict(x=x, meta=meta, g_mix_pre=g_mix_pre, w_in=w_in, b_f=b_f, g_sb=g_sb, g_fox=g_fox,
                  w_out=w_out, g_mix_post=g_mix_post, g_ffn_pre=g_ffn_pre, w_up=w_up, conv_w=conv_w,
                  conv_b=conv_b, w_down=w_down, g_ffn_post=g_ffn_post)
    batches = [c // 2 for c in range(8)]
    res = _run(inputs, 33, batches)
    out = np.stack([np.concatenate([np.asarray(res.results[2 * b]["y"], np.float32),
                                    np.asarray(res.results[2 * b + 1]["y"], np.float32)], axis=0)
                    for b in range(4)], axis=0)
    return out
```

```python
import numpy as np
import ml_dtypes
import concourse.bass as bass
import concourse.mybir as mybir
from concourse.bass_utils import run_bass_kernel_spmd

F32 = mybir.dt.float32
BF16 = mybir.dt.bfloat16
AF = mybir.ActivationFunctionType
ALU = mybir.AluOpType

D = 2048
KC = 16
NH = 16
HD = 128
NIN = 6152
DFF = 5632
NCH = 44
EPS = 1e-6
NMETA = 16
NPADROWS = 112
SAME_ENGINE_SYNC = True
NDMASEM = 20


class Prog:
    ENGS = ("pe", "act", "dve", "pool", "sp")

    def __init__(self):
        self.ops = []
        self.last_w = {}
        self.readers = {}
        self.final = []
        self.pending = {e: set() for e in self.ENGS}
        self.dma_since = []
        self.last_compute = {}

    def barrier(self):
        deps = set(self.last_compute.values()) | set(self.dma_since)
        for e in self.ENGS:
            self.pending[e] |= deps
        self.dma_since = []

    def add(self, eng, fn, reads=(), writes=(), dma=False, bg=False):
        i = len(self.ops)
        deps = set(self.pending[eng])
        self.pending[eng] = set()
        for b in reads:
            w = self.last_w.get(b)
            if w is not None:
                deps.add(w)
        for b in writes:
            w = self.last_w.get(b)
            if w is not None:
                deps.add(w)
            deps.update(self.readers.get(b, ()))
        keep = set()
        for d in deps:
            p = self.ops[d]
            if (not p["dma"]) and p["eng"] == eng:
                if eng == "pe" or not SAME_ENGINE_SYNC:
                    continue
            keep.add(d)
        self.ops.append(dict(eng=eng, fn=fn, dma=dma, deps=keep))
        if dma:
            if not bg:
                self.dma_since.append(i)
        else:
            self.last_compute[eng] = i
        for b in reads:
            self.readers.setdefault(b, []).append(i)
        for b in writes:
            self.last_w[b] = i
            self.readers[b] = []
        return i

    def emit(self, nc, block, engsem, dmasems):
        ops = self.ops
        dcount = {"sp": 0, "pool": 0}
        hist = {"sp": [], "pool": []}
        for i, op in enumerate(ops):
            if op["dma"]:
                q = op["eng"]
                k = dcount[q]
                dcount[q] += 1
                op["sem"] = dmasems[q][k % NDMASEM]
                op["val"] = 16 * (k // NDMASEM + 1)
                if k >= NDMASEM:
                    op["deps"].add(hist[q][k - NDMASEM])
                hist[q].append(i)
        flagged = set()
        for op in ops:
            for d in op["deps"]:
                if not ops[d]["dma"]:
                    flagged.add(d)
        cnt = {e: 0 for e in self.ENGS}
        for i, op in enumerate(ops):
            if (not op["dma"]) and i in flagged:
                cnt[op["eng"]] += 1
                op["sem"] = engsem[op["eng"]]
                op["val"] = cnt[op["eng"]]
        streams = {e: [] for e in self.ENGS}
        for i, op in enumerate(ops):
            streams[op["eng"]].append(i)
        final = self.final

        def run(engname, e):
            waited = {}
            for i in streams[engname]:
                op = ops[i]
                need = {}
                for d in op["deps"]:
                    p = ops[d]
                    s, v = p["sem"], p["val"]
                    key = id(s)
                    if key not in need or need[key][1] < v:
                        need[key] = (s, v)
                for key, (s, v) in need.items():
                    if waited.get(key, 0) >= v:
                        continue
                    e.wait_ge(s, v)
                    waited[key] = v
                ins = op["fn"](e)
                if op["dma"]:
                    ins.then_inc(op["sem"], 16)
                elif i in flagged:
                    ins.then_inc(op["sem"], 1)
            if engname == "sp":
                for d in final:
                    p = ops[d]
                    e.wait_ge(p["sem"], p["val"])

        @block.tensor
        def _(e):
            run("pe", e)

        @block.scalar
        def _(e):
            run("act", e)

        @block.vector
        def _(e):
            run("dve", e)

        @block.gpsimd
        def _(e):
            run("pool", e)

        @block.sync
        def _(e):
            run("sp", e)


def build_program(NB, NL=2):
    T = NB * 128
    nc = bass.Bass("TRN2", target_bir_lowering=False)
    P = Prog()

    def din(name, shape):
        return nc.dram_tensor(name, list(shape), F32, kind="ExternalInput").ap()

    xp = din("xp", [T, D])
    w_in = din("w_in", [NL, D, NIN])
    w_out = din("w_out", [NL, D, D])
    w_up = din("w_up", [NL, D, 2 * DFF])
    w_down = din("w_down", [NL, DFF, D])
    gains = din("gains", [NL, 4, 128, D])
    ghead = din("ghead", [NL, 128, NH])
    bfrep = din("bfrep", [NL, 128, NB * 8])
    convp = din("convp", [NL, 128, 2 * NCH * 4])
    cmat = din("cmat", [128, 11 * 128])
    selm = din("selm", [8, 8 * 128])
    rowm = din("rowm", [128, NB])
    biask = din("biask", [128, NB * 8])
    OWN0 = (NB - 1) // 2
    assert OWN0 % 4 == 0
    NOUT = NB - 1 - OWN0
    y = nc.dram_tensor("y", [NOUT * 128, D], F32, kind="ExternalOutput").ap()

    Hs = nc.dram_tensor("Hs", [T, D], F32).ap()
    QTd = nc.dram_tensor("QTd", [NH, 128, T], BF16).ap()
    KTd = nc.dram_tensor("KTd", [NH, 128, T], BF16).ap()
    Vd = nc.dram_tensor("Vd", [T, D], BF16).ap()
    OTd = nc.dram_tensor("OTd", [NH, 128, T], BF16).ap()
    U2Td = nc.dram_tensor("U2Td", [128, KC, T], BF16).ap()
    GTd = nc.dram_tensor("GTd", [NCH, 128, T], BF16).ap()
    FFd = nc.dram_tensor("FFd", [T, D], F32).ap()
    WQKb = nc.dram_tensor("WQKb", [NL, 32, 128, KC * 128], BF16).ap()
    WVb = nc.dram_tensor("WVb", [NL, 4, 128, KC * 512], BF16).ap()
    WFb = nc.dram_tensor("WFb", [NL, 128, KC * 8], BF16).ap()
    WOb = nc.dram_tensor("WOb", [NL, KC, 128, D], BF16).ap()
    WUPb = nc.dram_tensor("WUPb", [NL, 2 * NCH, 128, KC * 128], BF16).ap()
    WDNb = nc.dram_tensor("WDNb", [NL, 2, NCH // 4, 128, 4 * 1024], BF16).ap()

    ARENA_F = 46 * 1024
    CONST_F = 2304

    from contextlib import ExitStack
    with ExitStack() as es:
        arena_t = es.enter_context(nc.sbuf_tensor("arena", [128, ARENA_F], F32))
        const_t = es.enter_context(nc.sbuf_tensor("consts", [128, CONST_F], F32))
        psf = [es.enter_context(nc.psum_tensor(f"psf{i}", [128, 512], F32)) for i in range(6)]
        psb = [es.enter_context(nc.psum_tensor(f"psb{i}", [128, 1024], BF16)) for i in range(2)]
        engsem = {e: es.enter_context(nc.semaphore(f"sem_{e}")) for e in Prog.ENGS}
        dmasems = {q: [es.enter_context(nc.semaphore(f"dsem_{q}{i}")) for i in range(NDMASEM)]
                   for q in ("sp", "pool")}
        block = es.enter_context(nc.Block())

        arena_f = arena_t[:]
        arena_b = arena_t[:].bitcast(BF16)
        const_f = const_t[:]
        const_b = const_t[:].bitcast(BF16)

        class Carver:
            def __init__(self, base=0):
                self.off = base

            def f32(self, n):
                o = (self.off + 3) // 4
                self.off = (o + n) * 4
                assert self.off <= ARENA_F * 4, self.off
                return arena_f[:, o:o + n]

            def bf(self, n):
                o = (self.off + 1) // 2
                self.off = (o + n) * 2
                assert self.off <= ARENA_F * 4, self.off
                return arena_b[:, o:o + n]

        CB = const_b[:, 0:1280]
        SELB = const_b[0:8, 1280:2304]
        CFo = 1280
        CF = const_f[:, CFo:CFo + 256]
        MF = const_f[:, CFo + 256:CFo + 512]
        RM = const_f[:, CFo + 530:CFo + 530 + NB]
        E0F = const_f[:, CFo + 600:CFo + 728]
        P.add("sp", lambda e: e.dma_start(out=E0F, in_=cmat[:, 1280:1408]), writes=["E0F"], dma=True)
        IDENT = CB[:, 0:128]
        MASK_LE = CB[:, 128:256]
        NTRI_INCL = CB[:, 384:512]
        NTRI_STRICT = CB[:, 512:640]
        ONESB = CB[:, 640:768]
        ZEROB = CB[:, 896:1024]
        NEG_LT = CB[:, 1024:1152]
        NEG_LE = CB[:, 1152:1280]
        ONESF = CF[:, 0:128]
        TRILEF = CF[:, 128:256]
        MASK_LT_F = MF[:, 0:128]

        P.add("pool", lambda e: e.dma_start(out=CB, in_=cmat[:, 0:1280]), writes=["CB"], dma=True)
        P.add("pool", lambda e: e.dma_start(out=SELB, in_=selm), writes=["SELB"], dma=True)
        P.add("sp", lambda e: e.dma_start(out=CF[:, 0:128], in_=cmat[:, 640:768]), writes=["CF0"], dma=True)
        P.add("sp", lambda e: e.dma_start(out=CF[:, 128:256], in_=cmat[:, 768:896]), writes=["CF1"], dma=True)
        P.add("sp", lambda e: e.dma_start(out=MF[:, 0:128], in_=cmat[:, 256:384]), writes=["MF"], dma=True)
        P.add("sp", lambda e: e.dma_start(out=RM, in_=rowm), writes=["RM"], dma=True)

        ring = {"i": 0}

        def next_bank():
            b = ring["i"] % 6
            ring["i"] += 1
            return b

        groups = []
        b0 = 0
        while b0 < NB:
            nb = min(4, NB - b0)
            groups.append((b0, nb))
            b0 += nb

        def rstd_ops(ss, rs, tag, mul):
            P.add("act", lambda e: e.activation(out=rs, in_=ss, func=AF.Ln, scale=mul, bias=EPSB),
                  reads=[tag + "_ss", "EPSB"], writes=[tag + "_rs"])
            P.add("act", lambda e: e.activation(out=rs, in_=rs, func=AF.Exp, scale=-0.5),
                  reads=[tag + "_rs"], writes=[tag + "_rs"])

        EPSB = const_f[:, CFo + 520:CFo + 521]
        ONEB = const_f[:, CFo + 521:CFo + 522]
        P.add("dve", lambda e: e.memset(EPSB, EPS), writes=["EPSB"])
        P.add("dve", lambda e: e.memset(ONEB, 1.0), writes=["ONEB"])

        def qk_cols(h):
            fox_ = h >= 8
            hh_ = h - 8 if fox_ else h
            return (3072 if fox_ else 0) + hh_ * 128, (4096 if fox_ else 1024) + hh_ * 128

        for l in range(NL):
            wi = w_in[l].rearrange("(k p) c -> p k c", p=128)
            ci = 0
            for h in range(NH):
                for col in qk_cols(h):
                    P.add("pool", lambda e, l=l, ci=ci, col=col, wi=wi: e.dma_start(
                        out=WQKb[l, ci].rearrange("p (k c) -> p k c", k=KC), in_=wi[:, :, col:col + 128]),
                        writes=[("WQKb", l, ci)], dma=True, bg=True)
                    ci += 1
            for c4 in range(4):
                col = (2048 if c4 < 2 else 5120) + (c4 % 2) * 512
                P.add("pool", lambda e, l=l, c4=c4, col=col, wi=wi: e.dma_start(
                    out=WVb[l, c4].rearrange("p (k c) -> p k c", k=KC), in_=wi[:, :, col:col + 512]),
                    writes=[("WVb", l, c4)], dma=True, bg=True)
            P.add("pool", lambda e, l=l, wi=wi: e.dma_start(
                out=WFb[l].rearrange("p (k c) -> p k c", k=KC), in_=wi[:, :, 6144:6152]),
                writes=[("WFb", l)], dma=True, bg=True)
            wo = w_out[l].rearrange("(k p) n -> p k n", p=128)
            for k in range(KC):
                P.add("pool", lambda e, l=l, k=k, wo=wo: e.dma_start(out=WOb[l, k], in_=wo[:, k, :]),
                      writes=[("WOb", l, k)], dma=True, bg=True)
            wu_ = w_up[l].rearrange("(k p) c -> p k c", p=128)
            for i in range(NCH):
                for (ch, col) in ((i, i * 128), (NCH + i, DFF + i * 128)):
                    P.add("pool", lambda e, l=l, ch=ch, col=col, wu_=wu_: e.dma_start(
                        out=WUPb[l, ch].rearrange("p (k c) -> p k c", k=KC), in_=wu_[:, :, col:col + 128]),
                        writes=[("WUPb", l, ch)], dma=True, bg=True)
            wd_ = w_down[l].rearrange("(k p) n -> p k n", p=128)
            for half in range(2):
                for k4 in range(NCH // 4):
                    P.add("pool", lambda e, l=l, half=half, k4=k4, wd_=wd_: e.dma_start(
                        out=WDNb[l, half, k4].rearrange("p (k n) -> p k n", k=4),
                        in_=wd_[:, 4 * k4:4 * k4 + 4, half * 1024:(half + 1) * 1024]),
                        writes=[("WDNb", l, half, k4)], dma=True, bg=True)

        def layer(l):
            Hsrc = xp if l == 0 else Hs
            Hsrc_name = "xp" if l == 0 else "Hs"
            own0 = OWN0 if l == NL - 1 else 0
            ogroups = [(g0, nb) for (g0, nb) in groups if g0 >= own0]
            t_own = own0 * 128
            P.barrier()
            cv0 = Carver()
            UT = cv0.bf(KC * T).rearrange("p (k t) -> p k t", k=KC)
            mark = cv0.off
            cv = Carver(mark)
            hbuf = [cv.f32(D) for _ in range(2)]
            gt1 = cv.f32(D)
            ubuf = [cv.bf(D) for _ in range(2)]
            stat = cv.f32(8)
            P.add("sp", lambda e, l=l: e.dma_start(out=gt1, in_=gains[l, 0]), writes=["gt1"], dma=True)
            for b in range(NB):
                s = b % 2
                P.add("sp", lambda e, b=b, s=s: e.dma_start(out=hbuf[s], in_=Hsrc[b * 128:(b + 1) * 128, :]),
                      reads=[(Hsrc_name, b)], writes=[("hbuf", s)], dma=True)
                P.add("act", lambda e, s=s: e.activation(out=ubuf[s], in_=hbuf[s], func=AF.Square,
                                                         accum_out=stat[:, 2 * s:2 * s + 1]),
                      reads=[("hbuf", s)], writes=[("ubuf", s), "p1%d_ss" % s])
                rstd_ops(stat[:, 2 * s:2 * s + 1], stat[:, 2 * s + 1:2 * s + 2], "p1%d" % s, 1.0 / D)
                P.add("dve", lambda e, s=s: e.scalar_tensor_tensor(out=ubuf[s], in0=hbuf[s], scalar=stat[:, 2 * s + 1:2 * s + 2],
                                                                   in1=gt1, op0=ALU.mult, op1=ALU.mult),
                      reads=[("hbuf", s), "p1%d_rs" % s, "gt1"], writes=[("ubuf", s)])
                for half in range(2):
                    for j in range(8):
                        k = half * 8 + j
                        P.add("pe", lambda e, s=s, k=k, j=j, half=half: e.transpose(
                            out=psb[half][:, j * 128:(j + 1) * 128], in_=ubuf[s][:, k * 128:(k + 1) * 128],
                            identity=IDENT),
                            reads=[("ubuf", s), "CB"], writes=[("psb", half)])
                    dst = UT[:, half * 8:(half + 1) * 8, b * 128:(b + 1) * 128]
                    src = psb[half][:].rearrange("p (k t) -> p k t", k=8)
                    eng = "act" if half == 0 else "dve"
                    if eng == "act":
                        P.add("act", lambda e, dst=dst, src=src: e.copy(out=dst, in_=src),
                              reads=[("psb", half)], writes=[("UT", b, half)])
                    else:
                        P.add("dve", lambda e, dst=dst, src=src: e.tensor_copy(out=dst, in_=src),
                              reads=[("psb", half)], writes=[("UT", b, half)])
            UTALL = [("UT", b) for b in range(NB)]

            wq = [cv.bf(KC * 128).rearrange("p (k c) -> p k c", k=KC) for _ in range(2)]
            stg = [cv.bf(512) for _ in range(4)]
            sgi = 0
            w_in_l = w_in[l].rearrange("(k p) c -> p k c", p=128)
            ci = 0
            for h in range(NH):
                fox = h >= 8
                hh = h - 8 if fox else h
                qcol = (3072 if fox else 0) + hh * 128
                kcol = (4096 if fox else 1024) + hh * 128
                for kind, col in (("q", qcol), ("k", kcol)):
                    s = ci % 2
                    ci += 1
                    cur = ci - 1
                    for c2 in ([0, 1] if cur == 0 else [cur + 1]):
                        if c2 >= 2 * NH:
                            continue
                        P.add("sp", lambda e, c2=c2: e.dma_start(
                            out=wq[c2 % 2], in_=WQKb[l, c2].rearrange("p (k c) -> p k c", k=KC)),
                            reads=[("WQKb", l, c2)], writes=[("wq", c2 % 2)], dma=True)
                    for (g0, nb) in (ogroups if kind == "q" else groups):
                        n = nb * 128
                        t0 = g0 * 128
                        bk = next_bank()
                        for k in range(KC):
                            P.add("pe", lambda e, s=s, k=k, bk=bk, t0=t0, n=n: e.matmul(
                                out=psf[bk][:, 0:n], lhsT=wq[s][:, k, :], rhs=UT[:, k, t0:t0 + n],
                                start=(k == 0), stop=(k == KC - 1)),
                                reads=[("wq", s)] + [("UT", g0 + i, hf) for i in range(nb) for hf in (0, 1)], writes=[("psf", bk)])
                        g4 = sgi % 4
                        sgi += 1
                        if kind == "q":
                            P.add("act", lambda e, g4=g4, bk=bk, n=n: e.activation(
                                out=stg[g4][:, 0:n], in_=psf[bk][:, 0:n], func=AF.Copy, scale=float(HD ** -0.5)),
                                reads=[("psf", bk)], writes=[("stg", g4)])
                        else:
                            P.add("dve", lambda e, g4=g4, bk=bk, n=n: e.tensor_copy(
                                out=stg[g4][:, 0:n], in_=psf[bk][:, 0:n]),
                                reads=[("psf", bk)], writes=[("stg", g4)])
                        dstd = QTd if kind == "q" else KTd
                        P.add("sp", lambda e, g4=g4, dstd=dstd, h=h, t0=t0, n=n: e.dma_start(
                            out=dstd[h][:, t0:t0 + n], in_=stg[g4][:, 0:n]),
                            reads=[("stg", g4)], writes=[(kind + "T", h, g0)], dma=True)

            P.barrier()
            cv2 = Carver(mark)
            wv = [cv2.bf(KC * 512).rearrange("p (k c) -> p k c", k=KC) for _ in range(2)]
            vst = [cv2.bf(512) for _ in range(4)]
            vi = 0
            for c2 in range(4):
                s = c2 % 2
                P.add("sp", lambda e, s=s, c2=c2: e.dma_start(
                    out=wv[s], in_=WVb[l, c2].rearrange("p (k c) -> p k c", k=KC)),
                    reads=[("WVb", l, c2)], writes=[("wv", s)], dma=True)
                for b in range(NB):
                    bk = next_bank()
                    for k in range(KC):
                        P.add("pe", lambda e, s=s, k=k, bk=bk, b=b: e.matmul(
                            out=psf[bk][:, 0:512], lhsT=UT[:, k, b * 128:(b + 1) * 128], rhs=wv[s][:, k, :],
                            start=(k == 0), stop=(k == KC - 1)),
                            reads=[("wv", s), ("UT", b, 0), ("UT", b, 1)], writes=[("psf", bk)])
                    vs_ = vi % 4
                    vi += 1
                    if vi % 2:
                        P.add("act", lambda e, vs_=vs_, bk=bk: e.copy(out=vst[vs_], in_=psf[bk][:, 0:512]),
                              reads=[("psf", bk)], writes=[("vst", vs_)])
                    else:
                        P.add("dve", lambda e, vs_=vs_, bk=bk: e.tensor_copy(out=vst[vs_], in_=psf[bk][:, 0:512]),
                              reads=[("psf", bk)], writes=[("vst", vs_)])
                    P.add("sp", lambda e, vs_=vs_, b=b, c2=c2: e.dma_start(
                        out=Vd[b * 128:(b + 1) * 128, c2 * 512:(c2 + 1) * 512], in_=vst[vs_]),
                        reads=[("vst", vs_)], writes=[("V", b, c2)], dma=True)

            P.barrier()
            cv2 = Carver(mark)
            wf = cv2.bf(KC * 8).rearrange("p (k c) -> p k c", k=KC)
            cvk_base = ARENA_F * 4 - 4 * (6 * NB * 8 + 64) - 2 * T - 64
            cvk = Carver(cvk_base)
            NEGC = cvk.f32(NB * 8)
            NEGCB = cvk.f32(NB * 8)
            CT = cvk.bf(T)
            GH = cvk.f32(NH)
            fb = cv2.f32(NB * 8)
            bft = cv2.f32(NB * 8)
            cnb = cv2.bf(NB * 8)
            bkt = cv2.f32(NB * 8)
            P.add("sp", lambda e, l=l: e.dma_start(out=bft, in_=bfrep[l]), writes=["bft"], dma=True)
            P.add("sp", lambda e: e.dma_start(out=bkt, in_=biask), writes=["bkt"], dma=True)
            P.add("sp", lambda e, l=l: e.dma_start(out=GH, in_=ghead[l]), writes=["GH"], dma=True)
            P.add("sp", lambda e: e.dma_start(out=wf, in_=WFb[l].rearrange("p (k c) -> p k c", k=KC)),
                  reads=[("WFb", l)], writes=["wf"], dma=True)
            bkf = next_bank()
            for b in range(NB):
                for k in range(KC):
                    P.add("pe", lambda e, k=k, b=b: e.matmul(
                        out=psf[bkf][:, b * 8:(b + 1) * 8], lhsT=UT[:, k, b * 128:(b + 1) * 128], rhs=wf[:, k, :],
                        start=(k == 0), stop=(k == KC - 1)),
                        reads=["wf", ("UT", b, 0), ("UT", b, 1)], writes=[("psf", bkf)])
            P.add("dve", lambda e: e.tensor_tensor(out=fb, in0=psf[bkf][:, 0:NB * 8], in1=bft, op=ALU.add),
                  reads=[("psf", bkf), "bft"], writes=["fb"])
            P.add("act", lambda e: e.activation(out=fb, in_=fb, func=AF.Exp, scale=-1.0),
                  reads=["fb"], writes=["fb"])
            P.add("act", lambda e: e.activation(out=fb, in_=fb, func=AF.Ln, bias=ONEB),
                  reads=["fb", "ONEB"], writes=["fb"])
            bkc = next_bank()
            for b in range(NB):
                for b2 in range(b + 1):
                    P.add("pe", lambda e, b=b, b2=b2: e.matmul(
                        out=psf[bkc][:, b * 8:(b + 1) * 8], lhsT=(TRILEF if b2 == b else ONESF),
                        rhs=fb[:, b2 * 8:(b2 + 1) * 8], start=(b2 == 0), stop=(b2 == b)),
                        reads=["fb", "CF0", "CF1"], writes=[("psf", bkc)])
            P.add("dve", lambda e: e.tensor_copy(out=NEGC, in_=psf[bkc][:, 0:NB * 8]),
                  reads=[("psf", bkc)], writes=["NEGC"])
            P.add("dve", lambda e: e.tensor_tensor(out=NEGCB, in0=NEGC, in1=bkt, op=ALU.add),
                  reads=["NEGC", "bkt"], writes=["NEGCB"])
            P.add("dve", lambda e: e.tensor_scalar(out=cnb, in0=NEGC, scalar1=-1.0, scalar2=None, op0=ALU.mult),
                  reads=["NEGC"], writes=["cnb"])
            for b in range(NB):
                j = b % 8
                P.add("pe", lambda e, b=b, j=j: e.transpose(out=psb[0][0:8, j * 128:(j + 1) * 128],
                                                            in_=cnb[:, b * 8:(b + 1) * 8], identity=IDENT),
                      reads=["cnb", "CB"], writes=[("psb", 0)])
                if j == 7 or b == NB - 1:
                    bs = b - j
                    P.add("dve", lambda e, bs=bs, j=j: e.tensor_copy(out=CT[0:8, bs * 128:(bs + j + 1) * 128],
                                                                     in_=psb[0][0:8, 0:(j + 1) * 128]),
                          reads=[("psb", 0)], writes=["CT"])

            P.barrier()
            ca = Carver()
            KTs = [ca.bf(T) for _ in range(4)]
            QTs = [ca.bf(T) for _ in range(4)]
            Vs = [ca.bf(T).rearrange("p (b c) -> p b c", c=128) for _ in range(4)]
            OTst = [ca.bf(T) for _ in range(2)]
            eb = [[ca.f32(512) for _ in range(2)] for _ in range(2)]
            gb = [[ca.f32(512) for _ in range(2)] for _ in range(2)]
            spb = [[ca.bf(512) for _ in range(2)] for _ in range(2)]
            Ab = [[ca.bf(512) for _ in range(2)] for _ in range(2)]
            of32 = [ca.f32(512) for _ in range(2)]
            osq = [ca.f32(512) for _ in range(2)]
            rdb = [ca.f32(512) for _ in range(2)]
            rsb = [ca.f32(512) for _ in range(2)]
            pacc = [ca.f32(512) for _ in range(2)]
            BT = [[[ca.f32(NB) for _ in range(2)] for _ in range(2)] for _ in range(2)]
            REFS = [ca.f32(16) for _ in range(2)]
            NEGCBv = NEGCB.rearrange("p (b h) -> p b h", h=8)
            gcount = [0]
            assert ca.off <= cvk_base, (ca.off, cvk_base)
            PSV = [psf[0][:], psf[1][:], psf[2][:], psf[3][:], psf[4][:], psf[5][:],
                   psb[0][:].bitcast(F32), psb[1][:].bitcast(F32)]
            PSN = [("psf", 0), ("psf", 1), ("psf", 2), ("psf", 3), ("psf", 4), ("psf", 5), ("psb", 0), ("psb", 1)]
            SBK = [(0, 1), (4, 5)]
            XBK = [2, 6]
            OBK = [3, 7]
            VdT = Vd.rearrange("(b p) c -> p b c", p=128)
            def ld_head(hs, h):
                P.add("sp", lambda e: e.dma_start(out=KTs[hs], in_=KTd[h]),
                      reads=[("kT", h, g0_) for (g0_, _) in groups], writes=[("KTs", hs)], dma=True)
                P.add("sp", lambda e: e.dma_start(out=QTs[hs][:, t_own:T], in_=QTd[h][:, t_own:T]),
                      reads=[("qT", h, g0_) for (g0_, _) in ogroups], writes=[("QTs", hs)], dma=True)
                P.add("sp", lambda e: e.dma_start(out=Vs[hs], in_=VdT[:, :, h * 128:(h + 1) * 128]),
                      reads=[("V", b, h // 4) for b in range(NB)], writes=[("Vs", hs)], dma=True)

            for hp in range(NH // 2):
                heads = (2 * hp, 2 * hp + 1)
                fox = heads[0] >= 8
                ctx = []
                for hp2 in ([0, 1] if hp == 0 else [hp + 1]):
                    if hp2 >= NH // 2:
                        continue
                    for j2 in range(2):
                        ld_head((hp2 % 2) * 2 + j2, 2 * hp2 + j2)
                for j, h in enumerate(heads):
                    hs = (hp % 2) * 2 + j
                    hh = h - 8 if fox else h
                    continue_marker = None
                    ctx.append(dict(j=j, h=h, hs=hs, hh=hh, pc=0))
                for j, h in enumerate(()):
                    P.add("sp", lambda e, hs=hs, h=h: e.dma_start(out=KTs[hs], in_=KTd[h]),
                          reads=[("kT", h, g0_) for (g0_, _) in groups], writes=[("KTs", hs)], dma=True)
                    P.add("sp", lambda e, hs=hs, h=h: e.dma_start(out=QTs[hs][:, t_own:T], in_=QTd[h][:, t_own:T]),
                          reads=[("qT", h, g0_) for (g0_, _) in ogroups], writes=[("QTs", hs)], dma=True)
                    P.add("sp", lambda e, hs=hs, h=h: e.dma_start(out=Vs[hs], in_=VdT[:, :, h * 128:(h + 1) * 128]),
                          reads=[("V", b, h // 4) for b in range(NB)], writes=[("Vs", hs)], dma=True)
                    ctx.append(dict(j=j, h=h, hs=hs, hh=hh, pc=0))
                for (g0, nb) in ogroups:
                    N = nb * 128
                    q0 = g0 * 128
                    kmax = g0 + nb - 1
                    order = list(range(0, kmax + 1)) if fox else list(range(kmax, -1, -1))
                    if not fox:
                        for c in ctx:
                            for bk in (XBK[c["j"]], OBK[c["j"]]):
                                P.add("pe", lambda e, bk=bk, N=N, hs=c["hs"], q0=q0: e.matmul(
                                    out=PSV[bk][:, 0:N], lhsT=ZEROB, rhs=QTs[hs][:, q0:q0 + N], start=True, stop=False),
                                    reads=[("QTs", c["hs"]), "CB"], writes=[PSN[bk]])

                    gset = gcount[0] % 2
                    gcount[0] += 1
                    if fox:
                        xb0 = XBK[0]
                        nhalf = 2 if N > 256 else 1
                        for half in range(nhalf):
                            if half == 0:
                                bref = g0 + 1 if nb >= 2 else g0
                            else:
                                bref = g0 + 3 if nb == 4 else g0 + 2
                            P.add("pe", lambda e, half=half, bref=bref: e.matmul(
                                out=PSV[xb0][:, half * 8:(half + 1) * 8], lhsT=E0F, rhs=NEGC[:, bref * 8:(bref + 1) * 8],
                                start=True, stop=True),
                                reads=["E0F", "NEGC"], writes=[PSN[xb0]])
                        P.add("act", lambda e, gset=gset, nhalf=nhalf: e.copy(out=REFS[gset][:, 0:8 * nhalf],
                                                                               in_=PSV[xb0][:, 0:8 * nhalf]),
                              reads=[PSN[xb0]], writes=[("REFS", gset)])
                        for c in ctx:
                            for half in range(nhalf):
                                P.add("dve", lambda e, gset=gset, j=c["j"], hh=c["hh"], half=half: e.tensor_scalar(
                                    out=BT[gset][j][half], in0=NEGCBv[:, :, hh],
                                    scalar1=REFS[gset][:, half * 8 + hh:half * 8 + hh + 1], scalar2=None,
                                    op0=ALU.subtract),
                                    reads=["NEGCB", ("REFS", gset)], writes=[("BT", gset, c["j"], half)])

                    def s_op(c, kb, sb, q0=q0, N=N, g0=g0, fox=fox):
                        off = max(0, kb - g0) * 128
                        n = N - off
                        diag = kb >= g0
                        hs, hh = c["hs"], c["hh"]
                        P.add("pe", lambda e: e.matmul(
                            out=PSV[sb][:, 0:n], lhsT=KTs[hs][:, kb * 128:(kb + 1) * 128],
                            rhs=QTs[hs][:, q0 + off:q0 + N], start=True, stop=(not diag)),
                            reads=[("KTs", hs), ("QTs", hs)], writes=[PSN[sb]])
                        if diag:
                            mneg = NEG_LE if fox else NEG_LT
                            P.add("pe", lambda e: e.matmul(
                                out=PSV[sb][:, 0:128], lhsT=IDENT, rhs=mneg, start=False, stop=True),
                                reads=["CB"], writes=[PSN[sb]])

                    def stage1(c, idx, kb, N=N, g0=g0, fox=fox, order=order, gset=gset):
                        j, hs, hh = c["j"], c["hs"], c["hh"]
                        sl = (c["pc"] + idx) % 2
                        sb = SBK[j][sl]
                        if idx == 0:
                            s_op(c, kb, sb)
                        if idx + 1 < len(order):
                            s_op(c, order[idx + 1], SBK[j][(c["pc"] + idx + 1) % 2])
                        off = max(0, kb - g0) * 128
                        n = N - off
                        first = idx == 0
                        last = idx == len(order) - 1
                        xb, ob = XBK[j], OBK[j]
                        if not fox:
                            P.add("act", lambda e: e.activation(out=eb[j][sl][:, 0:n], in_=PSV[sb][:, 0:n], func=AF.Exp),
                                  reads=[PSN[sb]], writes=[("eb", j, sl)])
                            P.add("act", lambda e: e.activation(out=spb[j][sl][:, 0:n], in_=eb[j][sl][:, 0:n],
                                                                func=AF.Ln, bias=ONEB),
                                  reads=[("eb", j, sl), "ONEB"], writes=[("spb", j, sl)])
                            P.add("pe", lambda e: e.matmul(out=PSV[xb][:, off:N], lhsT=NTRI_INCL, rhs=spb[j][sl][:, 0:n],
                                                           start=False, stop=False),
                                  reads=[("spb", j, sl), "CB"], writes=[PSN[xb]])
                        else:
                            for half in range(2):
                                a0 = max(off, half * 256)
                                a1 = min(N, (half + 1) * 256)
                                if a1 <= a0:
                                    continue
                                bias = BT[gset][j][half][:, kb:kb + 1]
                                P.add("act", lambda e, a0=a0, a1=a1, bias=bias: e.activation(
                                    out=Ab[j][sl][:, a0 - off:a1 - off], in_=PSV[sb][:, a0 - off:a1 - off],
                                    func=AF.Exp, bias=bias),
                                    reads=[PSN[sb], ("BT", gset, j, half)], writes=[("Ab", j, sl, half)])
                            abr = [("Ab", j, sl, 0), ("Ab", j, sl, 1)]
                            P.add("pe", lambda e: e.matmul(out=PSV[ob][:, off:N], lhsT=Vs[hs][:, kb, :],
                                                           rhs=Ab[j][sl][:, 0:n], start=first, stop=last),
                                  reads=abr + [("Vs", hs)], writes=[PSN[ob]])
                            if first:
                                P.add("dve", lambda e: e.tensor_copy(out=pacc[j][:, off:N], in_=Ab[j][sl][:, 0:n]),
                                      reads=abr, writes=[("pacc", j)])
                            else:
                                P.add("dve", lambda e: e.tensor_tensor(out=pacc[j][:, off:N], in0=pacc[j][:, off:N],
                                                                       in1=Ab[j][sl][:, 0:n], op=ALU.add),
                                      reads=abr + [("pacc", j)], writes=[("pacc", j)])

                    def stage2(c, idx, kb, N=N, g0=g0, order=order):
                        j, hs = c["j"], c["hs"]
                        sl = (c["pc"] + idx) % 2
                        off = max(0, kb - g0) * 128
                        n = N - off
                        last = idx == len(order) - 1
                        xb, ob = XBK[j], OBK[j]
                        P.add("act", lambda e: e.activation(out=gb[j][sl][:, 0:n], in_=PSV[xb][:, off:N], func=AF.Exp),
                              reads=[PSN[xb]], writes=[("gb", j, sl)])
                        P.add("pe", lambda e: e.matmul(out=PSV[xb][:, off:N], lhsT=NTRI_STRICT, rhs=spb[j][sl][:, 0:n],
                                                       start=False, stop=last),
                              reads=[("spb", j, sl), "CB"], writes=[PSN[xb]])
                        P.add("dve", lambda e: e.tensor_tensor(out=Ab[j][sl][:, 0:n], in0=eb[j][sl][:, 0:n],
                                                               in1=gb[j][sl][:, 0:n], op=ALU.mult),
                              reads=[("eb", j, sl), ("gb", j, sl)], writes=[("Ab", j, sl)])
                        P.add("pe", lambda e: e.matmul(out=PSV[ob][:, off:N], lhsT=Vs[hs][:, kb, :],
                                                       rhs=Ab[j][sl][:, 0:n], start=False, stop=last),
                              reads=[("Ab", j, sl), ("Vs", hs)], writes=[PSN[ob]])

                    for idx, kb in enumerate(order):
                        for c in ctx:
                            stage1(c, idx, kb)
                        if not fox:
                            for c in ctx:
                                stage2(c, idx, kb)
                    for c in ctx:
                        c["pc"] += len(order)

                    for c in ctx:
                        j, hs, h = c["j"], c["hs"], c["h"]
                        xb, ob = XBK[j], OBK[j]
                        ssb = SBK[j][c["pc"] % 2]
                        if fox:
                            P.add("pe", lambda e, j=j, xb=xb, N=N: e.matmul(out=PSV[xb][:, 0:N], lhsT=ONESF,
                                                                            rhs=pacc[j][:, 0:N], start=True, stop=True),
                                  reads=[("pacc", j), "CF0"], writes=[PSN[xb]])
                            P.add("dve", lambda e, j=j, xb=xb, N=N: e.tensor_scalar(
                                out=rdb[j][:, 0:N], in0=PSV[xb][:, 0:N], scalar1=1e-30, scalar2=None, op0=ALU.max),
                                reads=[PSN[xb]], writes=[("rdb", j)])
                            P.add("dve", lambda e, j=j, N=N: e.reciprocal(out=rdb[j][:, 0:N], in_=rdb[j][:, 0:N]),
                                  reads=[("rdb", j)], writes=[("rdb", j)])
                            P.add("dve", lambda e, j=j, ob=ob, N=N: e.tensor_tensor(
                                out=of32[j][:, 0:N], in0=PSV[ob][:, 0:N], in1=rdb[j][:, 0:N], op=ALU.mult),
                                reads=[PSN[ob], ("rdb", j)], writes=[("of32", j)])
                        else:
                            P.add("act", lambda e, j=j, ob=ob, N=N: e.copy(out=of32[j][:, 0:N], in_=PSV[ob][:, 0:N]),
                                  reads=[PSN[ob]], writes=[("of32", j)])
                        P.add("dve", lambda e, j=j, N=N: e.tensor_tensor(out=osq[j][:, 0:N], in0=of32[j][:, 0:N],
                                                                         in1=of32[j][:, 0:N], op=ALU.mult),
                              reads=[("of32", j)], writes=[("osq", j)])
                        P.add("pe", lambda e, j=j, ssb=ssb, N=N: e.matmul(out=PSV[ssb][:, 0:N], lhsT=ONESF, rhs=osq[j][:, 0:N],
                                                                          start=True, stop=True),
                              reads=[("osq", j), "CF0"], writes=[PSN[ssb]])
                        P.add("act", lambda e, j=j, ssb=ssb, N=N: e.activation(out=rsb[j][:, 0:N], in_=PSV[ssb][:, 0:N],
                                                                               func=AF.Ln, scale=1.0 / HD, bias=EPSB),
                              reads=[PSN[ssb], "EPSB"], writes=[("rsb", j)])
                        P.add("act", lambda e, j=j, N=N: e.activation(out=rsb[j][:, 0:N], in_=rsb[j][:, 0:N],
                                                                      func=AF.Exp, scale=-0.5),
                              reads=[("rsb", j)], writes=[("rsb", j)])
                        P.add("dve", lambda e, j=j, N=N, q0=q0, h=h: e.scalar_tensor_tensor(
                            out=OTst[j][:, q0:q0 + N], in0=of32[j][:, 0:N], scalar=GH[:, h:h + 1], in1=rsb[j][:, 0:N],
                            op0=ALU.mult, op1=ALU.mult),
                            reads=[("of32", j), ("rsb", j), "GH"], writes=[("OTst", j)])
                for c in ctx:
                    j, h = c["j"], c["h"]
                    P.add("sp", lambda e, j=j, h=h: e.dma_start(out=OTd[h][:, t_own:T], in_=OTst[j][:, t_own:T]),
                          reads=[("OTst", j)], writes=[("OT", h)], dma=True)

            P.barrier()
            c3 = Carver()
            WO = c3.bf(KC * D).rearrange("p (k n) -> p k n", k=KC)
            otb = [c3.bf(NH * 128).rearrange("p (h t) -> p h t", h=NH) for _ in range(2)]
            u2b = [c3.bf(D) for _ in range(2)]
            u2t = [c3.bf(D).rearrange("p (k t) -> p k t", k=KC) for _ in range(2)]
            junk3 = c3.bf(D)
            hb3 = [c3.f32(D) for _ in range(2)]
            yt3 = c3.f32(D)
            g3a = c3.f32(D)
            g3b = c3.f32(D)
            st3 = c3.f32(16)
            mixsb = [c3.f32(D) for _ in range(2)]
            w_out_l = w_out[l].rearrange("(k p) n -> p k n", p=128)
            for k in range(KC):
                P.add("sp", lambda e, k=k: e.dma_start(out=WO[:, k, :], in_=WOb[l, k]),
                      reads=[("WOb", l, k)], writes=[("WO", k)], dma=True)
            WOALL = [("WO", k) for k in range(KC)]
            P.add("sp", lambda e, l=l: e.dma_start(out=g3a, in_=gains[l, 1]), writes=["g3a"], dma=True)
            P.add("sp", lambda e, l=l: e.dma_start(out=g3b, in_=gains[l, 2]), writes=["g3b"], dma=True)
            OTdv = OTd.rearrange("h d t -> d h t")
            def p3A(b):
                    s = b % 2
                    P.add("sp", lambda e, s=s, b=b: e.dma_start(out=otb[s], in_=OTdv[:, :, b * 128:(b + 1) * 128]),
                          reads=[("OT", h) for h in range(NH)], writes=[("otb", s)], dma=True)
                    P.add("sp", lambda e, s=s, b=b: e.dma_start(out=hb3[s], in_=Hsrc[b * 128:(b + 1) * 128, :]),
                          reads=[(Hsrc_name, b)], writes=[("hb3", s)], dma=True)
                    banks = []
                    for c in range(4):
                        bk = next_bank()
                        banks.append(bk)
                        for h in range(NH):
                            P.add("pe", lambda e, s=s, h=h, c=c, bk=bk: e.matmul(
                                out=psf[bk][:, 0:512], lhsT=otb[s][:, h, :], rhs=WO[:, h, c * 512:(c + 1) * 512],
                                start=(h == 0), stop=(h == NH - 1)),
                                reads=[("otb", s), ("WO", h)], writes=[("psf", bk)])
                        if c % 2 == 0:
                            P.add("act", lambda e, c=c, bk=bk, s=s: e.copy(
                                out=mixsb[s][:, c * 512:(c + 1) * 512], in_=psf[bk][:, 0:512]),
                                reads=[("psf", bk)], writes=[("mixsb", s, c)])
                        else:
                            P.add("dve", lambda e, c=c, bk=bk, s=s: e.tensor_copy(
                                out=mixsb[s][:, c * 512:(c + 1) * 512], in_=psf[bk][:, 0:512]),
                                reads=[("psf", bk)], writes=[("mixsb", s, c)])
                    return banks

            def p3B(b, banks):
                    s = b % 2
                    mixall = [("mixsb", s, c) for c in range(4)]
                    P.add("act", lambda e, s=s: e.activation(out=junk3, in_=mixsb[s], func=AF.Square,
                                                             accum_out=st3[:, 8 * s + 4:8 * s + 5]),
                          reads=mixall, writes=["junk3", "p3%d_ss" % s])
                    rstd_ops(st3[:, 8 * s + 4:8 * s + 5], st3[:, 8 * s + 5:8 * s + 6], "p3%d" % s, 1.0 / D)
                    P.add("dve", lambda e, s=s: e.scalar_tensor_tensor(
                        out=yt3, in0=mixsb[s], scalar=st3[:, 8 * s + 5:8 * s + 6],
                        in1=g3a, op0=ALU.mult, op1=ALU.mult),
                        reads=mixall + ["p3%d_rs" % s, "g3a"], writes=[("yt3", c) for c in range(4)])
                    P.add("dve", lambda e, s=s: e.tensor_tensor(out=hb3[s], in0=hb3[s], in1=yt3, op=ALU.add),
                          reads=[("hb3", s)] + [("yt3", c) for c in range(4)], writes=[("hb3", s)])
                    P.add("dve", lambda e, s=s, b=b: e.tensor_scalar(out=hb3[s], in0=hb3[s], scalar1=RM[:, b:b + 1],
                                                                     scalar2=None, op0=ALU.mult),
                          reads=[("hb3", s), "RM"], writes=[("hb3", s)])
                    P.add("sp", lambda e, s=s, b=b: e.dma_start(out=Hs[b * 128:(b + 1) * 128, :], in_=hb3[s]),
                          reads=[("hb3", s)], writes=[("Hs", b)], dma=True)
                    P.add("act", lambda e, s=s: e.activation(out=u2b[s], in_=hb3[s], func=AF.Square,
                                                             accum_out=st3[:, 8 * s + 6:8 * s + 7]),
                          reads=[("hb3", s)], writes=[("u2b", s), "p3b%d_ss" % s])
                    rstd_ops(st3[:, 8 * s + 6:8 * s + 7], st3[:, 8 * s + 7:8 * s + 8], "p3b%d" % s, 1.0 / D)
                    P.add("dve", lambda e, s=s: e.scalar_tensor_tensor(out=u2b[s], in0=hb3[s], scalar=st3[:, 8 * s + 7:8 * s + 8],
                                                                       in1=g3b, op0=ALU.mult, op1=ALU.mult),
                          reads=[("hb3", s), "p3b%d_rs" % s, "g3b"], writes=[("u2b", s)])
                    for half in range(2):
                        for j in range(8):
                            k = half * 8 + j
                            P.add("pe", lambda e, s=s, k=k, j=j, half=half: e.transpose(
                                out=psb[half][:, j * 128:(j + 1) * 128], in_=u2b[s][:, k * 128:(k + 1) * 128],
                                identity=IDENT),
                                reads=[("u2b", s), "CB"], writes=[("psb", half)])
                        dst = u2t[s][:, half * 8:(half + 1) * 8, :]
                        src = psb[half][:].rearrange("p (k t) -> p k t", k=8)
                        if half == 0:
                            P.add("act", lambda e, dst=dst, src=src: e.copy(out=dst, in_=src),
                                  reads=[("psb", half)], writes=[("u2t", s, half)])
                        else:
                            P.add("dve", lambda e, dst=dst, src=src: e.tensor_copy(out=dst, in_=src),
                                  reads=[("psb", half)], writes=[("u2t", s, half)])
                    P.add("sp", lambda e, s=s, b=b: e.dma_start(out=U2Td[:, :, b * 128:(b + 1) * 128], in_=u2t[s]),
                          reads=[("u2t", s, 0), ("u2t", s, 1)], writes=[("U2T", b)], dma=True)


            blocks3 = list(range(own0, NB))
            bank_of = {blocks3[0]: p3A(blocks3[0])}
            for bi, b in enumerate(blocks3):
                if bi + 1 < len(blocks3):
                    bank_of[blocks3[bi + 1]] = p3A(blocks3[bi + 1])
                p3B(b, bank_of[b])

            P.barrier()
            c4 = Carver()
            U2 = c4.bf(KC * T).rearrange("p (k t) -> p k t", k=KC)
            wg = [c4.bf(KC * 128).rearrange("p (k c) -> p k c", k=KC) for _ in range(2)]
            wu = [c4.bf(KC * 128).rearrange("p (k c) -> p k c", k=KC) for _ in range(2)]
            gst = [c4.bf(T) for _ in range(2)]
            ag = [c4.f32(514) for _ in range(2)]
            au = [c4.f32(514) for _ in range(2)]
            yg = c4.f32(512)
            yu = c4.f32(512)
            sg = c4.f32(512)
            CP = c4.f32(2 * NCH * 4).rearrange("p (c f) -> p c f", f=4)
            for k in range(KC):
                P.add("sp", lambda e, k=k: e.dma_start(out=U2[:, k, t_own:T], in_=U2Td[:, k, t_own:T]),
                      reads=[("U2T", b) for b in range(own0, NB)], writes=[("U2", k)], dma=True)
            U2ALL = [("U2", k) for k in range(KC)]
            P.add("sp", lambda e, l=l: e.dma_start(out=CP, in_=convp[l].rearrange("p (c f) -> p c f", f=4)),
                  writes=["CP"], dma=True)
            w_up_l = w_up[l].rearrange("(k p) c -> p k c", p=128)
            for i in range(NCH):
                s = i % 2
                for i2 in ([0, 1] if i == 0 else [i + 1]):
                    if i2 >= NCH:
                        continue
                    s2 = i2 % 2
                    P.add("sp", lambda e, s2=s2, i2=i2: e.dma_start(
                        out=wg[s2], in_=WUPb[l, i2].rearrange("p (k c) -> p k c", k=KC)),
                        reads=[("WUPb", l, i2)], writes=[("wg", s2)], dma=True)
                    P.add("sp", lambda e, s2=s2, i2=i2: e.dma_start(
                        out=wu[s2], in_=WUPb[l, NCH + i2].rearrange("p (k c) -> p k c", k=KC)),
                        reads=[("WUPb", l, NCH + i2)], writes=[("wu", s2)], dma=True)
                for gi, (g0, nb) in enumerate(ogroups):
                    n = nb * 128
                    t0 = g0 * 128
                    a = gi % 2
                    bg = next_bank()
                    bu = next_bank()
                    for (wt, wn, bk) in ((wg, "wg", bg), (wu, "wu", bu)):
                        for k in range(KC):
                            P.add("pe", lambda e, wt=wt, s=s, k=k, bk=bk, t0=t0, n=n: e.matmul(
                                out=psf[bk][:, 0:n], lhsT=wt[s][:, k, :], rhs=U2[:, k, t0:t0 + n],
                                start=(k == 0), stop=(k == KC - 1)),
                                reads=[(wn, s), ("U2", k)], writes=[("psf", bk)])
                    for (at, an, bk) in ((ag, "ag", bg), (au, "au", bu)):
                        if gi == 0:
                            P.add("dve", lambda e, at=at, a=a: e.memset(at[a][:, 0:2], 0.0), writes=[(an, a)])
                        else:
                            pn = ogroups[gi - 1][1] * 128
                            P.add("dve", lambda e, at=at, a=a, pn=pn: e.tensor_copy(out=at[a][:, 0:2],
                                                                                    in_=at[1 - a][:, pn:pn + 2]),
                                  reads=[(an, 1 - a)], writes=[(an, a)])
                        P.add("act", lambda e, at=at, a=a, bk=bk, n=n: e.copy(out=at[a][:, 2:2 + n], in_=psf[bk][:, 0:n]),
                              reads=[("psf", bk)], writes=[(an, a)])
                    for (at, an, yt, yn, ch) in ((ag, "ag", yg, "yg", i), (au, "au", yu, "yu", NCH + i)):
                        P.add("dve", lambda e, at=at, a=a, yt=yt, ch=ch, n=n: e.tensor_scalar(
                            out=yt[:, 0:n], in0=at[a][:, 2:2 + n], scalar1=CP[:, ch, 2:3], scalar2=CP[:, ch, 3:4],
                            op0=ALU.mult, op1=ALU.add),
                            reads=[(an, a), "CP"], writes=[yn])
                        P.add("dve", lambda e, at=at, a=a, yt=yt, ch=ch, n=n: e.scalar_tensor_tensor(
                            out=yt[:, 0:n], in0=at[a][:, 1:1 + n], scalar=CP[:, ch, 1:2], in1=yt[:, 0:n],
                            op0=ALU.mult, op1=ALU.add),
                            reads=[(an, a), "CP", yn], writes=[yn])
                        P.add("dve", lambda e, at=at, a=a, yt=yt, ch=ch, n=n: e.scalar_tensor_tensor(
                            out=yt[:, 0:n], in0=at[a][:, 0:n], scalar=CP[:, ch, 0:1], in1=yt[:, 0:n],
                            op0=ALU.mult, op1=ALU.add),
                            reads=[(an, a), "CP", yn], writes=[yn])
                    P.add("act", lambda e, n=n: e.activation(out=sg[:, 0:n], in_=yg[:, 0:n], func=AF.Silu),
                          reads=["yg"], writes=["sg"])
                    P.add("dve", lambda e, s=s, t0=t0, n=n: e.tensor_tensor(out=gst[s][:, t0:t0 + n], in0=sg[:, 0:n],
                                                                           in1=yu[:, 0:n], op=ALU.mult),
                          reads=["sg", "yu"], writes=[("gst", s)])
                P.add("sp", lambda e, s=s, i=i: e.dma_start(out=GTd[i][:, t_own:T], in_=gst[s][:, t_own:T]),
                      reads=[("gst", s)], writes=[("GT", i)], dma=True)

            P.barrier()
            c5 = Carver()
            WD = c5.bf(NCH * 1024).rearrange("p (k n) -> p k n", k=NCH)
            gtb = [c5.bf(NCH * 256).rearrange("p (k t) -> p k t", k=NCH) for _ in range(2)]
            ffs = [c5.f32(1024) for _ in range(2)]
            ffb = [c5.f32(D) for _ in range(2)]
            hb6 = [c5.f32(D) for _ in range(2)]
            g6 = c5.f32(D)
            st6 = c5.f32(8)
            junk6 = ffs[0].bitcast(BF16)
            GTdv = GTd.rearrange("i c t -> c i t")
            P.add("sp", lambda e, l=l: e.dma_start(out=g6, in_=gains[l, 3]), writes=["g6"], dma=True)
            lastl = l == NL - 1
            for half in range(2):
                for k4 in range(0, NCH, 4):
                    P.add("sp", lambda e, k4=k4, half=half: e.dma_start(
                        out=WD[:, k4:k4 + 4, :], in_=WDNb[l, half, k4 // 4].rearrange("p (k n) -> p k n", k=4)),
                        reads=[("WDNb", l, half, k4 // 4)], writes=[("WD", k4)], dma=True)
                for b in range(own0, NB):
                    s = ((b - own0) // 2) % 2
                    jb = (b - own0) % 2
                    fs = b % 2
                    if jb == 0:
                        for bb in ([b, b + 2] if b == own0 else [b + 2]):
                            if bb >= NB:
                                continue
                            sx = ((bb - own0) // 2) % 2
                            nb2 = min(2, NB - bb)
                            P.add("sp", lambda e, sx=sx, bb=bb, nb2=nb2: e.dma_start(
                                out=gtb[sx][:, :, 0:nb2 * 128], in_=GTdv[:, :, bb * 128:(bb + nb2) * 128]),
                                reads=[("GT", i) for i in range(NCH)], writes=[("gtb", sx)], dma=True)
                    if half == 1:
                        P.add("sp", lambda e, fs=fs, b=b: e.dma_start(out=ffb[fs][:, 0:1024], in_=FFd[b * 128:(b + 1) * 128, 0:1024]),
                              reads=[("FF", b, 0)], writes=[("ffb", fs, "lo")], dma=True)
                        P.add("sp", lambda e, fs=fs, b=b: e.dma_start(out=hb6[fs], in_=Hs[b * 128:(b + 1) * 128, :]),
                              reads=[("Hs", b)], writes=[("hb6", fs)], dma=True)
                    for c in range(2):
                        bk = next_bank()
                        for k in range(NCH):
                            P.add("pe", lambda e, s=s, k=k, c=c, bk=bk, jb=jb: e.matmul(
                                out=psf[bk][:, 0:512], lhsT=gtb[s][:, k, jb * 128:(jb + 1) * 128],
                                rhs=WD[:, k, c * 512:(c + 1) * 512],
                                start=(k == 0), stop=(k == NCH - 1)),
                                reads=[("gtb", s), ("WD", (k // 4) * 4)], writes=[("psf", bk)])
                        if half == 0:
                            dst, dname = ffs[fs][:, c * 512:(c + 1) * 512], ("ffs", fs, c)
                        else:
                            dst, dname = ffb[fs][:, 1024 + c * 512:1024 + (c + 1) * 512], ("ffb", fs, "hi", c)
                        if c == 0:
                            P.add("act", lambda e, dst=dst, bk=bk: e.copy(out=dst, in_=psf[bk][:, 0:512]),
                                  reads=[("psf", bk)], writes=[dname])
                        else:
                            P.add("dve", lambda e, dst=dst, bk=bk: e.tensor_copy(out=dst, in_=psf[bk][:, 0:512]),
                                  reads=[("psf", bk)], writes=[dname])
                    if half == 0:
                        P.add("sp", lambda e, fs=fs, b=b: e.dma_start(
                            out=FFd[b * 128:(b + 1) * 128, 0:1024], in_=ffs[fs]),
                            reads=[("ffs", fs, 0), ("ffs", fs, 1)], writes=[("FF", b, 0)], dma=True)
                        continue
                    ffall = [("ffb", fs, "lo"), ("ffb", fs, "hi", 0), ("ffb", fs, "hi", 1)]
                    P.add("act", lambda e, fs=fs: e.activation(out=junk6, in_=ffb[fs], func=AF.Square,
                                                               accum_out=st6[:, 2 * fs:2 * fs + 1]),
                          reads=ffall, writes=["junk6", ("ffs", 0, 0), ("ffs", 0, 1), "p6%d_ss" % fs])
                    rstd_ops(st6[:, 2 * fs:2 * fs + 1], st6[:, 2 * fs + 1:2 * fs + 2], "p6%d" % fs, 1.0 / D)
                    P.add("dve", lambda e, fs=fs: e.tensor_tensor(out=ffb[fs], in0=ffb[fs], in1=g6, op=ALU.mult),
                          reads=ffall + ["g6"], writes=ffall + [("ffb", fs, "all")])
                    P.add("dve", lambda e, fs=fs: e.scalar_tensor_tensor(out=hb6[fs], in0=ffb[fs], scalar=st6[:, 2 * fs + 1:2 * fs + 2],
                                                                         in1=hb6[fs], op0=ALU.mult, op1=ALU.add),
                          reads=ffall + [("ffb", fs, "all"), ("hb6", fs), "p6%d_rs" % fs], writes=[("hb6", fs)])
                    P.add("dve", lambda e, fs=fs, b=b: e.tensor_scalar(out=hb6[fs], in0=hb6[fs], scalar1=RM[:, b:b + 1],
                                                                       scalar2=None, op0=ALU.mult),
                          reads=[("hb6", fs), "RM"], writes=[("hb6", fs)])
                    if lastl:
                        if b >= own0 + 1:
                            i = P.add("sp", lambda e, fs=fs, b=b: e.dma_start(
                                out=y[(b - own0 - 1) * 128:(b - own0) * 128, :], in_=hb6[fs]),
                                      reads=[("hb6", fs)], writes=[("y", b)], dma=True)
                            P.final.append(i)
                    else:
                        P.add("sp", lambda e, fs=fs, b=b: e.dma_start(out=Hs[b * 128:(b + 1) * 128, :], in_=hb6[fs]),
                              reads=[("hb6", fs)], writes=[("Hs", b)], dma=True)

        for l in range(NL):
            layer(l)
        P.emit(nc, block, engsem, dmasems)
    return nc


def _consts():
    j = np.arange(128)[:, None]
    t = np.arange(128)[None, :]
    ident = (j == t).astype(np.float32)
    mask_le = (j <= t).astype(np.float32)
    mask_lt = (j < t).astype(np.float32)
    ntri_incl = -(j >= t).astype(np.float32)
    ntri_strict = -(j < t).astype(np.float32)
    ones = np.ones((128, 128), np.float32)
    tri_le = (j <= t).astype(np.float32)
    zeros = np.zeros((128, 128), np.float32)
    neg_lt = np.where(j < t, 0.0, -30000.0).astype(np.float32)
    neg_le = np.where(j <= t, 0.0, -30000.0).astype(np.float32)
    e0 = np.zeros((128, 128), np.float32)
    e0[0, :] = 1.0
    cm = np.concatenate([ident, mask_le, mask_lt, ntri_incl, ntri_strict, ones, tri_le, zeros, neg_lt, neg_le, e0], axis=1)
    sel = np.zeros((8, 8, 128), np.float32)
    for h in range(8):
        sel[h, h, :] = 1.0
    return cm, sel.reshape(8, 1024)


_PROG_CACHE = {}


def _run(inputs, NB, batches, NL=2):
    x = np.asarray(inputs["x"], np.float32)
    meta = np.asarray(inputs["meta"], np.float32)
    T = NB * 128
    nreal = (NB - 1) * 128
    cm, sel = _consts()
    cm_dev = cm.copy()
    gains = np.stack([np.stack([np.broadcast_to(np.asarray(inputs[k], np.float32)[l][None, :], (128, D))
                                for k in ("g_mix_pre", "g_mix_post", "g_ffn_pre", "g_ffn_post")])
                      for l in range(NL)]).astype(np.float32)
    ghead = np.stack([np.concatenate([np.asarray(inputs["g_sb"], np.float32)[l],
                                      np.asarray(inputs["g_fox"], np.float32)[l]], axis=0).T
                      for l in range(NL)]).astype(np.float32)
    bfrep = np.stack([np.broadcast_to(np.tile(np.asarray(inputs["b_f"], np.float32)[l], NB)[None, :], (128, NB * 8))
                      for l in range(NL)]).astype(np.float32)
    cw = np.asarray(inputs["conv_w"], np.float32)
    cb = np.asarray(inputs["conv_b"], np.float32)
    convp = np.zeros((NL, 128, 2 * NCH, 4), np.float32)
    for l in range(NL):
        for k in range(3):
            convp[l, :, :, k] = cw[l, k].reshape(2 * NCH, 128).T
        convp[l, :, :, 3] = cb[l].reshape(2 * NCH, 128).T
    convp = convp.reshape(NL, 128, 2 * NCH * 4)
    common = dict(
        w_in=np.ascontiguousarray(np.asarray(inputs["w_in"], np.float32)[:NL]),
        w_out=np.ascontiguousarray(np.asarray(inputs["w_out"], np.float32)[:NL]),
        w_up=np.ascontiguousarray(np.asarray(inputs["w_up"], np.float32)[:NL]),
        w_down=np.ascontiguousarray(np.asarray(inputs["w_down"], np.float32)[:NL]),
        gains=gains, ghead=ghead, bfrep=bfrep, convp=convp, cmat=cm_dev, selm=sel)
    own0 = (NB - 1) // 2
    in_maps = []
    for c in range(8):
        b = batches[c]
        role = c % 2
        xpad = np.zeros((T, D), np.float32)
        rm = np.ones((128, NB), np.float32)
        if role == 1:
            xpad[NPADROWS:128] = meta
            xpad[128:] = x[b, :nreal]
            rm[:NPADROWS, 0] = 0.0
        else:
            o = own0 * 128
            xpad[o + NPADROWS:o + 128] = meta
            xpad[o + 128:] = x[b, :nreal - o]
            rm[:, :own0] = 0.0
            rm[:NPADROWS, own0] = 0.0
        bk = np.repeat(np.where(rm == 0.0, -30000.0, 0.0).astype(np.float32), 8, axis=1)
        m = dict(common)
        m["xp"] = xpad
        m["rowm"] = rm
        m["biask"] = np.ascontiguousarray(bk)
        in_maps.append(m)
    key = (NB, NL)
    if key not in _PROG_CACHE:
        _PROG_CACHE[key] = build_program(NB, NL)
    nc = _PROG_CACHE[key]
    res = run_bass_kernel_spmd(nc, in_maps, core_ids=list(range(8)))
    return res


def kernel(x, meta, g_mix_pre, w_in, b_f, g_sb, g_fox, w_out, g_mix_post, g_ffn_pre, w_up, conv_w,
           conv_b, w_down, g_ffn_post):
    inputs = dict(x=x, meta=meta, g_mix_pre=g_mix_pre, w_in=w_in, b_f=b_f, g_sb=g_sb, g_fox=g_fox,
                  w_out=w_out, g_mix_post=g_mix_post, g_ffn_pre=g_ffn_pre, w_up=w_up, conv_w=conv_w,
                  conv_b=conv_b, w_down=w_down, g_ffn_post=g_ffn_post)
    batches = [c // 2 for c in range(8)]
    res = _run(inputs, 33, batches)
    out = np.stack([np.concatenate([np.asarray(res.results[2 * b]["y"], np.float32),
                                    np.asarray(res.results[2 * b + 1]["y"], np.float32)], axis=0)
                    for b in range(4)], axis=0)
    return out
```
